# Optimizing a Trainium2 kernel written in Bass

```python
import jax, jax.numpy as jnp
from jax import lax
import numpy as np

D_MODEL = 2048
BATCH = 8
SEQ = 2048
DEPTH = 2

POOL_WIDTH = D_MODEL // 2
POOL_WINDOWS = (2, 4, 8, 16)
POOL_GROUPS = 4
POOL_GROUP_DIM = POOL_WIDTH // POOL_GROUPS
GMLP_WIDTH = D_MODEL // 2
GMLP_HEADS = 8
GMLP_HEAD_DIM = GMLP_WIDTH // GMLP_HEADS
GMLP_CHUNK = 128
SSM_INNER = D_MODEL
SSM_HEAD_DIM = 64
SSM_HEADS = SSM_INNER // SSM_HEAD_DIM
SSM_GROUPS = 8
SSM_STATE = 128
SSM_CONV = 4
SSM_CHUNK = 128
SSM_CONV_DIM = SSM_INNER + 2 * SSM_GROUPS * SSM_STATE
N_BRANCHES = 3
D_FF = 4 * D_MODEL
N_ADA = 6
IN_SIZES = (POOL_WIDTH, 2 * GMLP_WIDTH, SSM_INNER, SSM_CONV_DIM, SSM_HEADS, N_BRANCHES * D_MODEL)
IN_PROJ_DIM = POOL_WIDTH + 2 * GMLP_WIDTH + SSM_INNER + SSM_CONV_DIM + SSM_HEADS + N_BRANCHES * D_MODEL
IN_SPLITS = (POOL_WIDTH,
             POOL_WIDTH + 2 * GMLP_WIDTH,
             POOL_WIDTH + 2 * GMLP_WIDTH + SSM_INNER,
             POOL_WIDTH + 2 * GMLP_WIDTH + SSM_INNER + SSM_CONV_DIM,
             POOL_WIDTH + 2 * GMLP_WIDTH + SSM_INNER + SSM_CONV_DIM + SSM_HEADS)
RMS_EPS = 1e-6
LN_EPS = 1e-5

kernel_name = 'hybrid_pool_gmlp_ssd_adaln_block'


def rms_norm(x, eps=RMS_EPS):
    xf = x.astype(jnp.float32)
    return (xf * lax.rsqrt(jnp.mean(xf * xf, axis=-1, keepdims=True) + eps)).astype(x.dtype)


def layer_norm(x, g, b, eps=LN_EPS):
    xf = x.astype(jnp.float32)
    mu = jnp.mean(xf, axis=-1, keepdims=True)
    var = jnp.mean(jnp.square(xf - mu), axis=-1, keepdims=True)
    return ((xf - mu) * lax.rsqrt(var + eps)).astype(x.dtype) * g + b


def pool_mixer(p, w_group, scale):
    bsz, s, _ = p.shape
    pf = p.astype(jnp.float32).reshape(bsz, s, POOL_GROUPS, POOL_GROUP_DIM)
    cs = jnp.concatenate([jnp.zeros((bsz, 1, POOL_GROUPS, POOL_GROUP_DIM), jnp.float32),
                          jnp.cumsum(pf, axis=1)], axis=1)
    t = jnp.arange(s)
    means = []
    for g, w in enumerate(POOL_WINDOWS):
        cs_g = cs[:, :, g]
        lo = jnp.maximum(t + 1 - w, 0)
        win_sum = cs_g[:, 1:] - jnp.take(cs_g, lo, axis=1)
        count = jnp.minimum(t + 1, w).astype(jnp.float32)
        means.append(win_sum / count[None, :, None])
    pooled = (jnp.stack(means, axis=2) - pf).astype(p.dtype)
    mixed = jnp.einsum('bsgc,gcd->bsgd', pooled, w_group).reshape(bsz, s, POOL_WIDTH)
    return mixed * scale


def gmlp_mixer(uv, ln_g, ln_b, w_s, b_s):
    zz = jax.nn.gelu(uv, approximate=False)
    u, v = jnp.split(zz, 2, axis=-1)
    v = layer_norm(v, ln_g, ln_b)
    bsz, s, _ = v.shape
    nc = s // GMLP_CHUNK
    vh = v.reshape(bsz, nc, GMLP_CHUNK, GMLP_HEADS, GMLP_HEAD_DIM)
    causal = jnp.tril(jnp.ones((GMLP_CHUNK, GMLP_CHUNK), dtype=bool))
    w_masked = jnp.where(causal[None], w_s, jnp.zeros_like(w_s))
    mixed = jnp.einsum('hts,bnshd->bnthd', w_masked, vh) + b_s.T[:, :, None]
    return u * mixed.reshape(bsz, s, GMLP_WIDTH)


def causal_depthwise_conv(x, w, bias):
    out = lax.conv_general_dilated(x, w[:, None, :], window_strides=(1,),
                                   padding=[(SSM_CONV - 1, 0)],
                                   dimension_numbers=('NWC', 'WIO', 'NWC'),
                                   feature_group_count=x.shape[-1])
    return out + bias


def ssd_chunked(x, dt, a, b_mat, c_mat):
    bsz, s, h, p = x.shape
    L = SSM_CHUNK
    nc = s // L
    G, R, N = SSM_GROUPS, h // SSM_GROUPS, SSM_STATE
    xd = (x * dt[..., None]).reshape(bsz, nc, L, G, R, p)
    da = (dt * a).reshape(bsz, nc, L, h).transpose(0, 3, 1, 2)
    a_cs = jnp.cumsum(da, axis=-1)
    bc = b_mat.reshape(bsz, nc, L, G, N)
    cc = c_mat.reshape(bsz, nc, L, G, N)
    causal = jnp.tril(jnp.ones((L, L), dtype=bool))
    diff = a_cs[..., :, None] - a_cs[..., None, :]
    decay = jnp.exp(jnp.where(causal, diff, -jnp.inf)).reshape(bsz, G, R, nc, L, L)
    cb = jnp.einsum('bclgn,bcsgn->bgcls', cc, bc)
    y_diag = jnp.einsum('bgrcls,bcsgrp->bclgrp', cb[:, :, None] * decay, xd)
    decay_states = jnp.exp(a_cs[..., -1:] - a_cs).reshape(bsz, G, R, nc, L)
    states = jnp.einsum('bclgn,bgrcl,bclgrp->bcgrpn', bc, decay_states, xd)
    chunk_decay = jnp.exp(a_cs[..., -1]).reshape(bsz, G, R, nc)

    def step(state, inp):
        st, dec = inp
        return state * dec[..., None, None] + st, state

    h0 = jnp.zeros((bsz, G, R, p, N), jnp.float32)
    _, prev = lax.scan(step, h0, (jnp.moveaxis(states, 1, 0), jnp.moveaxis(chunk_decay, -1, 0)))
    prev = jnp.moveaxis(prev, 0, 1)
    decay_out = jnp.exp(a_cs).reshape(bsz, G, R, nc, L)
    y_off = jnp.einsum('bclgn,bcgrpn,bgrcl->bclgrp', cc, prev, decay_out)
    return (y_diag + y_off).reshape(bsz, s, h, p)


def ssm_mixer(z, xbc, dt_raw, conv_w, conv_b, dt_bias, a_log, d_skip, norm_g):
    bsz, s, _ = z.shape
    xbc = jax.nn.silu(causal_depthwise_conv(xbc, conv_w, conv_b)).astype(jnp.float32)
    xs, bm, cm = jnp.split(xbc, (SSM_INNER, SSM_INNER + SSM_GROUPS * SSM_STATE), axis=-1)
    xs = xs.reshape(bsz, s, SSM_HEADS, SSM_HEAD_DIM)
    bm = bm.reshape(bsz, s, SSM_GROUPS, SSM_STATE)
    cm = cm.reshape(bsz, s, SSM_GROUPS, SSM_STATE)
    dt = jax.nn.softplus(dt_raw.astype(jnp.float32) + dt_bias.astype(jnp.float32))
    a = -jnp.exp(a_log.astype(jnp.float32))
    y = ssd_chunked(xs, dt, a, bm, cm) + d_skip.astype(jnp.float32)[:, None] * xs
    y = y.reshape(bsz, s, SSM_INNER) * jax.nn.silu(z.astype(jnp.float32))
    yg = y.reshape(bsz, s, SSM_GROUPS, SSM_INNER // SSM_GROUPS)
    yg = yg * lax.rsqrt(jnp.mean(yg * yg, axis=-1, keepdims=True) + RMS_EPS)
    return (yg.reshape(bsz, s, SSM_INNER) * norm_g.astype(jnp.float32)).astype(z.dtype)


def setup_inputs(seed: int = 0) -> dict:
    key = jax.random.key(seed)
    ks = jax.random.split(key, 32)
    f32 = jnp.float32
    nrm = lambda k, shape, scale: jax.random.normal(k, shape, f32) * scale
    dt0 = jnp.exp(jax.random.uniform(ks[12], (DEPTH, SSM_HEADS), f32, np.log(1e-3), np.log(1e-1)))
    return {
        'x': nrm(ks[0], (BATCH, SEQ, D_MODEL), 1.0),
        'c': nrm(ks[1], (BATCH, D_MODEL), 1.0),
        'w_ada': nrm(ks[2], (DEPTH, D_MODEL, N_ADA * D_MODEL), D_MODEL ** -0.5),
        'b_ada': nrm(ks[3], (DEPTH, N_ADA * D_MODEL), 0.02),
        'w_in': nrm(ks[4], (DEPTH, D_MODEL, IN_PROJ_DIM), D_MODEL ** -0.5),
        'pool_w': nrm(ks[5], (DEPTH, POOL_GROUPS, POOL_GROUP_DIM, POOL_GROUP_DIM), POOL_GROUP_DIM ** -0.5),
        'pool_scale': 1.0 + nrm(ks[6], (DEPTH, POOL_WIDTH), 0.02),
        'gmlp_ln_g': 1.0 + nrm(ks[7], (DEPTH, GMLP_WIDTH), 0.02),
        'gmlp_ln_b': nrm(ks[8], (DEPTH, GMLP_WIDTH), 0.02),
        'gmlp_ws': nrm(ks[9], (DEPTH, GMLP_HEADS, GMLP_CHUNK, GMLP_CHUNK), GMLP_CHUNK ** -0.5),
        'gmlp_bs': 1.0 + nrm(ks[10], (DEPTH, GMLP_HEADS, GMLP_CHUNK), 0.02),
        'conv_w': nrm(ks[11], (DEPTH, SSM_CONV, SSM_CONV_DIM), SSM_CONV ** -0.5),
        'conv_b': nrm(ks[13], (DEPTH, SSM_CONV_DIM), 0.02),
        'dt_bias': dt0 + jnp.log(-jnp.expm1(-dt0)),
        'a_log': jnp.log(jax.random.uniform(ks[14], (DEPTH, SSM_HEADS), f32, 1.0, 16.0)),
        'd_skip': 1.0 + nrm(ks[15], (DEPTH, SSM_HEADS), 0.02),
        'ssm_norm': 1.0 + nrm(ks[16], (DEPTH, SSM_INNER), 0.02),
        'w_pool_out': nrm(ks[17], (DEPTH, POOL_WIDTH, D_MODEL), POOL_WIDTH ** -0.5),
        'w_gmlp_out': nrm(ks[18], (DEPTH, GMLP_WIDTH, D_MODEL), GMLP_WIDTH ** -0.5),
        'w_ssm_out': nrm(ks[19], (DEPTH, SSM_INNER, D_MODEL), SSM_INNER ** -0.5),
        'w_o': nrm(ks[20], (DEPTH, D_MODEL, D_MODEL), D_MODEL ** -0.5),
        'w_up': nrm(ks[21], (DEPTH, D_MODEL, D_FF), D_MODEL ** -0.5),
        'w_down': nrm(ks[22], (DEPTH, D_FF, D_MODEL), D_FF ** -0.5),
        'final_norm': 1.0 + nrm(ks[23], (D_MODEL,), 0.02),
    }


def reference(x, c, w_ada, b_ada, w_in, pool_w, pool_scale, gmlp_ln_g, gmlp_ln_b, gmlp_ws, gmlp_bs,
              conv_w, conv_b, dt_bias, a_log, d_skip, ssm_norm, w_pool_out, w_gmlp_out, w_ssm_out,
              w_o, w_up, w_down, final_norm):
    c_act = jax.nn.silu(c)
    for l in range(DEPTH):
        ada = (c_act @ w_ada[l] + b_ada[l])[:, None, :]
        sh1, sc1, g1, sh2, sc2, g2 = jnp.split(ada, N_ADA, axis=-1)
        h = rms_norm(x) * (1.0 + sc1) + sh1
        proj = h @ w_in[l]
        p, uv, z, xbc, dt_raw, gates = jnp.split(proj, IN_SPLITS, axis=-1)
        y_a = pool_mixer(p, pool_w[l], pool_scale[l]) @ w_pool_out[l]
        y_b = gmlp_mixer(uv, gmlp_ln_g[l], gmlp_ln_b[l], gmlp_ws[l], gmlp_bs[l]) @ w_gmlp_out[l]
        y_c = ssm_mixer(z, xbc, dt_raw, conv_w[l], conv_b[l], dt_bias[l], a_log[l], d_skip[l],
                        ssm_norm[l]) @ w_ssm_out[l]
        ga, gb, gc = jnp.split(jax.nn.sigmoid(gates), N_BRANCHES, axis=-1)
        merged = ga * y_a + gb * y_b + gc * y_c
        x = x + g1 * (merged @ w_o[l])
        h = rms_norm(x) * (1.0 + sc2) + sh2
        x = x + g2 * (jnp.square(jax.nn.relu(h @ w_up[l])) @ w_down[l])
    return rms_norm(x) * final_norm
```

```python
import numpy as np
import concourse.bass as bass
import concourse.mybir as mybir
from concourse.bass_utils import run_bass_kernel_spmd

F32 = mybir.dt.float32
BF16 = mybir.dt.bfloat16
AF = mybir.ActivationFunctionType
ALU = mybir.AluOpType

D = 2048
S = 2048
DEPTH = 2
T = 512
NT = S // T
KC = D // 128
NCH = T // 128
IN_PROJ = 15392
C_P, C_U, C_V, C_Z, C_X, C_B, C_C, C_DT, C_G = 0, 1024, 2048, 3072, 5120, 7168, 8192, 9216, 9248
RMS_EPS = 1e-6
LN_EPS = 1e-5
NWS = 4


PAGE = 64
ESZ = {F32: 4, BF16: 2}


class Op:
    __slots__ = ("eng", "fn", "deps", "dma_sem", "dma_val", "sig", "sigval")


class Sched:
    ENGS = ("pe", "act", "dve", "pool", "sp")

    def __init__(self):
        self.ops = {e: [] for e in self.ENGS}
        self.last_w = {}
        self.readers = {}
        self.dma_sems = {}
        self.same_engine_sync = True
        self.bases = {}
        self.pcache = {}
        self.bank_last = {}

    def pages(self, ap):
        name = ap.tensor.name
        b = self.bases.get(name)
        if b is None:
            assert ap.space not in ("SB", "PSUM"), name
            return ()
        key = (name, ap.ap, ap.offset)
        r = self.pcache.get(key)
        if r is not None:
            return r
        space, base = b
        es = ESZ[ap.dtype]
        dims = ap.ap
        pstride = dims[0][0]
        foff = ap.offset % pstride if pstride > 0 else ap.offset
        free = list(dims[1:])
        if not free:
            free = [(1, 1)]
        lstep, lcnt = free[-1]
        outer = free[:-1]
        starts = [foff]
        for (st, cnt) in outer:
            if st == 0:
                continue
            starts = [s0 + st * i for s0 in starts for i in range(cnt)]
        pg = set()
        span = (lcnt - 1) * lstep + 1
        for s0 in starts:
            lo = (base + s0 * es) // PAGE
            hi = (base + (s0 + span) * es - 1) // PAGE
            for p in range(lo, hi + 1):
                pg.add((space, p))
        r = tuple(pg)
        self.pcache[key] = r
        return r

    def add(self, eng, fn, reads=(), writes=(), dma=None):
        op = Op()
        op.eng = eng
        op.fn = fn
        op.deps = set()
        op.sig = False
        op.sigval = 0
        op.dma_sem = None
        op.dma_val = 0
        if dma is not None:
            n = self.dma_sems.get(dma, 0) + 1
            self.dma_sems[dma] = n
            op.dma_sem = dma
            op.dma_val = 16 * n
        rp = set()
        wp = set()
        for a in reads:
            if a is not None and not isinstance(a, (int, float)):
                rp.update(self.pages(a))
        for a in writes:
            if a is not None:
                wp.update(self.pages(a))
        deps = op.deps
        for p in rp:
            w = self.last_w.get(p)
            if w is not None:
                deps.add(w)
        for p in wp:
            w = self.last_w.get(p)
            if w is not None:
                deps.add(w)
            rd = self.readers.get(p)
            if rd:
                deps.update(rd.values())
        banks = set()
        for (sp_, p) in rp:
            if sp_ == "P":
                banks.add(p * PAGE // 2048)
        for (sp_, p) in wp:
            if sp_ == "P":
                banks.add(p * PAGE // 2048)
        for bk in banks:
            bl = self.bank_last.get(bk)
            if bl is None:
                bl = self.bank_last[bk] = {}
            for e2, o2 in bl.items():
                if e2 != eng:
                    deps.add(o2)
            bl[eng] = op
        deps.discard(op)
        rkey = dma if dma is not None else eng
        for p in rp:
            d = self.readers.get(p)
            if d is None:
                d = self.readers[p] = {}
            d[rkey] = op
        for p in wp:
            self.last_w[p] = op
            self.readers[p] = {}
        self.ops[eng].append(op)
        return op

    def emit(self, nc):
        for e in self.ENGS:
            for op in self.ops[e]:
                for d in op.deps:
                    if d.dma_sem is None:
                        if d.eng == op.eng and (d.eng == "pe" or not self.same_engine_sync):
                            continue
                        d.sig = True
        sems = {}
        for e in self.ENGS:
            sems[e] = nc.alloc_semaphore("sem_" + e)
            c = 0
            for op in self.ops[e]:
                if op.dma_sem is None and op.sig:
                    c += 1
                    op.sigval = c
        dsems = {k: nc.alloc_semaphore("dsem_%d" % i) for i, k in enumerate(self.dma_sems)}
        sched = self

        def run_engine(e, engobj):
            waited = {}
            for op in sched.ops[e]:
                need = {}
                for d in op.deps:
                    if d.dma_sem is not None:
                        key = ("d", d.dma_sem)
                        val = d.dma_val
                    else:
                        if d.eng == e and (e == "pe" or not sched.same_engine_sync):
                            continue
                        key = ("e", d.eng)
                        val = d.sigval
                    if val > need.get(key, 0):
                        need[key] = val
                for key, val in need.items():
                    if waited.get(key, 0) >= val:
                        continue
                    waited[key] = val
                    sem = dsems[key[1]] if key[0] == "d" else sems[key[1]]
                    engobj.wait_ge(sem, val)
                ins = op.fn(engobj)
                if op.dma_sem is not None:
                    ins.then_inc(dsems[op.dma_sem], 16)
                elif op.sig:
                    ins.then_inc(sems[e], 1)
            if e == "sp":
                for k, n in sched.dma_sems.items():
                    engobj.wait_ge(dsems[k], 16 * n)

        with nc.Block() as block:
            @block.tensor
            def _(eng):
                run_engine("pe", eng)

            @block.scalar
            def _(eng):
                run_engine("act", eng)

            @block.vector
            def _(eng):
                run_engine("dve", eng)

            @block.gpsimd
            def _(eng):
                run_engine("pool", eng)

            @block.sync
            def _(eng):
                run_engine("sp", eng)


def bc(ap, axis, shape):
    return ap.unsqueeze(axis).to_broadcast(list(shape))


class Builder:
    def __init__(self, ntiles=NT, layers=(0, 1), final=True, taps=(), stop_after=None):
        self.ntiles = ntiles
        self.layers = layers
        self.final = final
        self.taps = set(taps)
        self.stop_after = stop_after
        self.nc = bass.Bass("TRN2", target_bir_lowering=False)
        self.s = Sched()
        self.ps_i = 0
        self.ws_i = 0
        self.tap_outs = {}
        self.sb_off = 16512 + 0
        self.stopped = False
        self.blk_ctr = 0

    def sb_at(self, name, shape, dt, off):
        n = 1
        for d in shape[1:]:
            n *= d
        nb = n * ESZ[dt]
        assert off % 64 == 0 and off + nb <= 229344, (name, off, nb)
        t = self.nc.alloc_sbuf_tensor_at(name, list(shape), dt, offset=off)
        self.s.bases[t.name] = ("S", off)
        return t, nb

    def sb(self, name, shape, dt=F32):
        t, nb = self.sb_at(name, shape, dt, self.sb_off)
        self.sb_off += (nb + 63) // 64 * 64
        return t

    def din(self, name, shape, dt=F32):
        return self.nc.dram_tensor(name, list(shape), dt, kind="ExternalInput").ap()

    def dout(self, name, shape, dt=F32):
        return self.nc.dram_tensor(name, list(shape), dt, kind="ExternalOutput").ap()

    def psn(self):
        ring = self.ps_ring
        self.ps_i = (self.ps_i + 1) % len(ring)
        return self.ps[ring[self.ps_i]]

    def op(self, eng, fn, reads=(), writes=(), dma=None):
        if self.stopped:
            return
        self.s.add(eng, fn, reads, writes, dma)

    def tap(self, name, ap, shape, dt=F32):
        if name not in self.taps or name in self.tap_outs or self.stopped:
            return
        o = self.dout("dbg_" + name, shape, dt)
        self.tap_outs[name] = (shape, dt)
        self.op("sp", lambda e: e.dma_start(out=o, in_=ap), reads=[ap], dma="tap_" + name)

    def stop(self, name):
        if self.stop_after == name:
            self.stopped = True

    def mm(self, out, lhsT, rhs, start, stop):
        rd = [lhsT, rhs] if start else [lhsT, rhs, out]
        self.op("pe", lambda e: e.matmul(out, lhsT, rhs, start=start, stop=stop), rd, [out])

    def tr(self, out, in_, ident):
        self.op("pe", lambda e: e.transpose(out, in_, ident), [in_, ident], [out])

    def act(self, out, in_, func, bias=None, scale=None, accum_out=None):
        kw = {}
        if bias is not None:
            kw["bias"] = bias
        if scale is not None:
            kw["scale"] = scale
        if accum_out is not None:
            kw["accum_out"] = accum_out
        wr = [out] if accum_out is None else [out, accum_out]
        self.op("act", lambda e: e.activation(out, in_, func, **kw), [in_, bias, scale], wr)

    def tt(self, out, in0, in1, op, eng="dve"):
        self.op(eng, lambda e: e.tensor_tensor(out, in0, in1, op), [in0, in1], [out])

    def ts(self, out, in0, s1, s2, op0, op1=None, eng="dve"):
        if op1 is None:
            self.op(eng, lambda e: e.tensor_scalar(out, in0, s1, None, op0), [in0, s1], [out])
        else:
            self.op(eng, lambda e: e.tensor_scalar(out, in0, s1, s2, op0, op1), [in0, s1, s2], [out])

    def stt(self, out, in0, scalar, in1, op0, op1, eng="dve"):
        self.op(eng, lambda e: e.scalar_tensor_tensor(out, in0, scalar, in1, op0, op1), [in0, scalar, in1], [out])

    def cp(self, out, in_, eng="dve"):
        if eng == "act":
            self.act(out, in_, AF.Identity)
            return
        self.op(eng, lambda e: e.tensor_copy(out, in_), [in_], [out])

    def recip(self, out, in_):
        self.op("dve", lambda e: e.reciprocal(out, in_), [in_], [out])

    def memset(self, ap, val, eng="pool"):
        self.op(eng, lambda e: e.memset(ap, val), [], [ap])

    def dma(self, eng, out, in_, key):
        self.op(eng, lambda e: e.dma_start(out=out, in_=in_), [in_], [out], dma=key)

    def wslot(self):
        i = self.ws_i
        self.ws_i = (i + 1) % NWS
        return i, self.wslots[i]

    def wload(self, src, nk, ncol):
        i, slot = self.wslot()
        self.dma("pool", slot[:, 0:nk * ncol], src, "ws%d" % i)
        return slot[:, 0:nk * ncol].rearrange("p (k c) -> p k c", k=nk)

    def fm(self, wv, nk, rhs, nblk=2):
        banks = []
        for m in range(nblk):
            pb = self.psn()
            for kc in range(nk):
                self.mm(pb[:, :], wv[:, kc, m * 128:(m + 1) * 128], rhs(kc), kc == 0, kc == nk - 1)
            banks.append(pb)
        return banks

    def tm(self, wv, nk, ncol):
        banks = [self.psn(), self.psn()]
        for c in range(NCH):
            reg = banks[c // 2][:, (c % 2) * 256:(c % 2) * 256 + ncol]
            for kc in range(nk):
                self.mm(reg, self.hT[:, kc, c * 128:(c + 1) * 128], wv[:, kc, 0:ncol], kc == 0, kc == nk - 1)
        return banks

    def build(self):
        nc = self.nc
        L2 = DEPTH
        self.xT_d = self.din("xT", [D, S])
        self.c_d = self.din("c_col", [128, KC])
        self.w_ada = self.din("w_ada_t", [L2, 48, 128, 4096])
        self.b_ada = self.din("b_ada_col", [L2, 128, 96])
        self.w_in = self.din("w_in_t", [L2, 60, 128, 4096])
        self.w_dt = self.din("w_dt_t", [L2, 128, KC * 32])
        self.pool_w = self.din("pool_w_t", [L2, 4, 128, 512])
        self.prm_d = self.din("prm", [L2, 128, 320])
        self.wst_d = self.din("gmlp_wsT", [L2, 128, 8, 128])
        self.bsb_d = self.din("gmlp_bs_bc", [L2, 128, 8, 128])
        self.w_pool_out = self.din("w_pool_out_t", [L2, 8, 128, 2048])
        self.w_gmlp_out = self.din("w_gmlp_out_t", [L2, 8, 128, 2048])
        self.w_ssm_out = self.din("w_ssm_out_t", [L2, 8, 128, 4096])
        self.w_o = self.din("w_o_t", [L2, 8, 128, 4096])
        self.w_up = self.din("w_up_t", [L2, 32, 128, 4096])
        self.w_down = self.din("w_down_t", [L2, 8, 4, 128, 4096])
        self.fn_d = self.din("final_norm_col", [128, KC])
        self.out_d = self.dout("outT", [D, S])

        sb = self.sb
        self.xT = sb("xT_sb", [128, KC, T])
        self.hT = sb("hT_sb", [128, KC, T], BF16)
        self.wslots = [sb("ws%d" % i, [128, 4096], BF16) for i in range(NWS)]
        self.ones_f = sb("ones_f", [128, 128])
        self.ident_f = sb("ident_f", [128, 128])
        self.ident_b = sb("ident_b", [128, 128], BF16)
        self.lmask = sb("lmask", [128, 128])
        self.umask = sb("umask", [128, 128])
        self.cst = sb("cst", [128, 8])
        self.sq = [sb("sq%d" % i, [128, T]) for i in range(2)]
        self.rstd = sb("rstd", [128, T])
        self.cact = sb("cact", [128, KC], BF16)
        self.ccol = sb("ccol", [128, KC])
        self.ada = sb("ada", [128, L2, 96])
        self.prm = sb("prm_sb", [128, L2, 320])
        self.fncol = sb("fncol", [128, KC])
        self.Wt = sb("Wt", [128, L2, 8, 128], BF16)
        self.R = sb("Rg", [128, L2, 8, 128])
        self.state = sb("state", [128, L2, 2048])
        self.ccarry = sb("ccarry", [128, L2, 32, 3])
        self.pcarry = sb("pcarry", [128, L2, 8, 16])
        self.invc = sb("invc", [128, 4, 16])
        self.abc = sb("abc", [128, L2, 32])
        self.dts = sb("dts", [128, NCH, 32])
        self.da = sb("da", [128, NCH, 32])
        self.e3 = sb("e3", [128, 3, NCH, 32])
        self.dtds = sb("dtds", [128, NCH, 32])
        self.sm = sb("sm", [128, 16])
        S0 = self.sb_off
        self.S0 = S0
        avail = 229344 - S0
        assert avail >= 84000, avail
        at = lambda name, shape, dt, off: self.sb_at(name, shape, dt, S0 + off)[0]
        self.yn = at("yn", [128, KC, T], BF16, 0)
        self.gm = at("gm", [128, 8, T], BF16, 16384)
        self.mixed = at("mixed", [128, 8, T], BF16, 24576)
        X = 32768
        o = X
        def nxt(name, shape, dt):
            nonlocal o
            t, nb = self.sb_at(name, shape, dt, S0 + o)
            o += (nb + 63) // 64 * 64
            return t
        self.raw = [nxt("raw%d" % i, [128, 516], F32) for i in range(2)]
        self.cacc = [nxt("cacc%d" % i, [128, T], F32) for i in range(4)]
        self.xsF = nxt("xsF", [128, 2, T], BF16)
        self.xsT = [nxt("xsT%d" % i, [128, NCH, 256], BF16) for i in range(2)]
        self.BT = [nxt("BT%d" % i, [128, NCH, 128], BF16) for i in range(2)]
        self.BF = [nxt("BF%d" % i, [128, T], BF16) for i in range(2)]
        self.CF = [nxt("CF%d" % i, [128, T], BF16) for i in range(2)]
        self.zs = [nxt("zs%d" % i, [128, NCH, 256], BF16) for i in range(2)]
        self.cbm = [nxt("cbm%d" % i, [128, 128], F32) for i in range(2)]
        self.rdal = [nxt("rdal%d" % i, [128, 4, 128], F32) for i in range(2)]
        self.ex = [nxt("ex%d" % i, [128, 4, 128], F32) for i in range(2)]
        self.MT = [nxt("MT%d" % i, [128, 4, 128], BF16) for i in range(2)]
        self.xd = [nxt("xd%d" % i, [128, 4, 64], BF16) for i in range(2)]
        self.xdds = [nxt("xdds%d" % i, [128, 4, 64], BF16) for i in range(2)]
        self.yt1 = [nxt("yt1_%d" % i, [128, 4, 64], F32) for i in range(2)]
        self.yt2 = [nxt("yt2_%d" % i, [128, 4, 64], F32) for i in range(2)]
        self.yg = [nxt("yg%d" % i, [128, 256], F32) for i in range(2)]
        self.ynT = [nxt("ynT%d" % i, [128, 256], F32) for i in range(2)]
        self.junk = nxt("junk", [128, 256], BF16)
        self.stbf = [nxt("stbf%d" % i, [128, 256], BF16) for i in range(2)]
        assert o <= avail, (o, avail)
        o = X
        self.uF = nxt("uF", [128, 8, T], BF16)
        self.vg = nxt("vg", [128, NCH, 1024], F32)
        self.vn = nxt("vn", [128, NCH, 1024], BF16)
        self.gtmp = [nxt("gtmp%d" % i, [128, NCH, 128], F32) for i in range(2)]
        self.bnst = nxt("bnst", [128, 2, 8], F32)
        assert o <= avail, (o, avail)
        o = X
        self.praw = nxt("praw", [128, 2, 528], F32)
        self.pA = nxt("pA", [128, 2, 528], F32)
        self.pB = nxt("pB", [128, 2, 528], F32)
        self.pooled = [nxt("pooled%d" % i, [128, 2, T], BF16) for i in range(2)]
        self.p16 = nxt("p16", [128, 2, 16], F32)
        o = X
        self.mg = nxt("mg", [128, KC, T], BF16)
        self.gsb = [nxt("gsb%d" % i, [128, 2, T], F32) for i in range(2)]
        self.macc = nxt("macc", [128, 2, T], F32)
        self.mtmp = [nxt("mtmp%d" % i, [128, 2, T], F32) for i in range(2)]
        assert o <= avail, (o, avail)
        o = X
        self.tmpW = nxt("tmpW", [128, 8, 128], F32)
        self.tmpB = nxt("tmpB", [128, 8, 128], F32)
        self.hid = at("hid", [128, 64, T], BF16, 0)
        self.rtmp = [at("rtmp%d" % i, [128, T], F32, 65536 + i * 2048) for i in range(2)]

        self.ps_ring = list(range(7))
        self.stats_ready = False
        self.ps = [nc.alloc_psum_tensor("ps%d" % i, [128, 512], F32) for i in range(8)]
        for i in range(8):
            self.s.bases[self.ps[i].name] = ("P", i * 2048)

        self.setup()
        for ti in range(self.ntiles):
            self.load_x(ti)
            for l in self.layers:
                self.layer(ti, l)
            if self.final:
                self.final_norm(ti)
            self.store_out(ti)
        self.stopped = False
        with nc.allow_low_precision("bf16 matmul operands by design"):
            self.s.emit(nc)
        return nc

    def setup(self):
        ms = self.memset
        ms(self.ones_f[:, :], 1.0)
        ms(self.cst[:, 0:1], RMS_EPS)
        ms(self.cst[:, 1:2], LN_EPS)
        ms(self.cst[:, 2:3], 1.0)
        ms(self.cst[:, 3:4], 0.0)
        ms(self.lmask[:, :], 1.0)
        self.op("pool", lambda e: e.affine_select(out=self.lmask[:, :], in_=self.lmask[:, :], pattern=[[1, 128]],
                                                  compare_op=ALU.is_ge, fill=0.0, base=0, channel_multiplier=-1),
                [self.lmask[:, :]], [self.lmask[:, :]])
        ms(self.umask[:, :], 1.0)
        self.op("pool", lambda e: e.affine_select(out=self.umask[:, :], in_=self.umask[:, :], pattern=[[-1, 128]],
                                                  compare_op=ALU.is_gt, fill=0.0, base=0, channel_multiplier=1),
                [self.umask[:, :]], [self.umask[:, :]])
        ms(self.ident_f[:, :], 1.0)
        self.op("pool", lambda e: e.affine_select(out=self.ident_f[:, :], in_=self.ident_f[:, :], pattern=[[-1, 128]],
                                                  compare_op=ALU.is_equal, fill=0.0, base=0, channel_multiplier=1),
                [self.ident_f[:, :]], [self.ident_f[:, :]])
        self.cp(self.ident_b[:, :], self.ident_f[:, :])
        ms(self.state[:, :, :], 0.0)
        ms(self.ccarry[:, :, :, :], 0.0)
        ms(self.pcarry[:, :, :, :], 0.0)
        for g in range(4):
            w = 2 << g
            ms(self.invc[:, g, :], 1.0 / w)
            for j in range(w - 1):
                ms(self.invc[:, g, j:j + 1], 1.0 / (j + 1))
        self.dma("sp", self.ccol[:, :], self.c_d, "small1")
        self.dma("sp", self.prm[:, :, :], self.prm_d.rearrange("l p n -> p l n"), "small2")
        self.dma("sp", self.fncol[:, :], self.fn_d, "small3")
        self.dma("sp", self.ada[:, :, :], self.b_ada.rearrange("l p n -> p l n"), "small4")
        self.act(self.cact[:, :], self.ccol[:, :], AF.Silu)
        l0 = self.layers[0]
        for sl in range(16):
            self.ada_slab(l0, sl)
        self.ada_fix1(l0)
        if len(self.layers) == 1:
            for sl in range(16, 48):
                self.ada_slab(l0, sl)
            self.ada_fix2(l0)
        for l in range(DEPTH):
            self.act(self.abc[:, l, :], self.prm[:, l, 216:248], AF.Exp)
            self.ts(self.abc[:, l, :], self.abc[:, l, :], -1.0, None, ALU.mult)
            self.dma("sp", self.tmpW[:, :, :], self.wst_d[l], "small5")
            self.dma("sp", self.tmpB[:, :, :], self.bsb_d[l], "small6")
            self.tt(self.tmpW[:, :, :], self.tmpW[:, :, :], bc(self.lmask[:, :], 1, [128, 8, 128]), ALU.mult)
            self.cp(self.Wt[:, l, :, :], self.tmpW[:, :, :])
            for half in range(2):
                pr = self.psn()
                self.mm(pr[:, :], self.ones_f[:, :],
                        self.tmpW[:, half * 4:(half + 1) * 4, :].rearrange("p h t -> p (h t)"), True, True)
                for hh in range(4):
                    h = half * 4 + hh
                    self.stt(self.R[:, l, h, :], pr[:, hh * 128:(hh + 1) * 128], self.prm[:, l, 176 + h:177 + h],
                             self.tmpB[:, h, :], ALU.mult, ALU.add)
        self.tap("ada", self.ada[:, :, :], [128, DEPTH, 96])
        self.tap("R", self.R[:, :, :, :], [128, DEPTH, 8, 128])

    def load_x(self, ti):
        self.stats_ready = False
        src = self.xT_d[:, ti * T:(ti + 1) * T].rearrange("(kc p) t -> p kc t", p=128)
        self.dma("sp", self.xT[:, :, :], src, "xin")

    def store_out(self, ti):
        dst = self.out_d[:, ti * T:(ti + 1) * T].rearrange("(kc p) t -> p kc t", p=128)
        self.dma("sp", dst, self.xT[:, :, :], "xout")

    def stat_acc(self, blk):
        j = blk % 2
        self.act(self.sq[j][:, :], self.xT[:, blk, :], AF.Square)
        self.mm(self.ps[7][:, :], self.ones_f[:, :], self.sq[j][:, :], blk == 0, blk == KC - 1)

    def norm_stats(self):
        if not self.stats_ready:
            for kc in range(KC):
                self.stat_acc(kc)
        self.stats_ready = False
        self.act(self.rstd[:, :], self.ps[7][:, :], AF.Sqrt, bias=self.cst[:, 0:1], scale=1.0 / D)
        self.recip(self.rstd[:, :], self.rstd[:, :])

    def norm_mod(self, l, sc_ofs, sh_ofs):
        self.norm_stats()
        for kc in range(KC):
            j = kc % 2
            self.tt(self.sq[j][:, :], self.xT[:, kc, :], self.rstd[:, :], ALU.mult)
            self.act(self.hT[:, kc, :], self.sq[j][:, :], AF.Identity,
                     bias=self.ada[:, l, sh_ofs + kc:sh_ofs + kc + 1], scale=self.ada[:, l, sc_ofs + kc:sc_ofs + kc + 1])

    def final_norm(self, ti):
        self.tap("ada_end", self.ada[:, :, :], [128, DEPTH, 96])
        self.norm_stats()
        for kc in range(KC):
            j = kc % 2
            self.tt(self.sq[j][:, :], self.xT[:, kc, :], self.rstd[:, :], ALU.mult)
            self.act(self.xT[:, kc, :], self.sq[j][:, :], AF.Identity, scale=self.fncol[:, kc:kc + 1])

    def layer(self, ti, l):
        self.norm_mod(l, 16, 0)
        self.tap("h1", self.hT[:, :, :], [128, KC, T], BF16)
        self.stop("h1")
        self.ssm(ti, l)
        self.tap("yn", self.yn[:, :, :], [128, KC, T], BF16)
        self.stop("ssm")
        self.gmlp(ti, l)
        self.tap("gm", self.gm[:, :, :], [128, 8, T], BF16)
        self.stop("gmlp")
        self.pool(ti, l)
        self.tap("mixed", self.mixed[:, :, :], [128, 8, T], BF16)
        self.stop("pool")
        self.merge(ti, l)
        self.tap("mg", self.mg[:, :, :], [128, KC, T], BF16)
        self.stop("merge")
        self.wo(ti, l)
        self.tap("x1", self.xT[:, :, :], [128, KC, T])
        self.stop("wo")
        self.norm_mod(l, 64, 48)
        self.ffn(ti, l)
        self.tap("x2", self.xT[:, :, :], [128, KC, T])
        self.stop("ffn")

    def ada_slab(self, l, sl):
        wv = self.wload(self.w_ada[l, sl], KC, 256)
        pb = self.psn()
        for mb in range(2):
            for kc in range(KC):
                self.mm(pb[:, mb:mb + 1], wv[:, kc, mb * 128:(mb + 1) * 128], self.cact[:, kc:kc + 1],
                        kc == 0, kc == KC - 1)
        fb = sl * 2
        self.tt(self.ada[:, l, fb:fb + 2], pb[:, 0:2], self.ada[:, l, fb:fb + 2], ALU.add)

    def ada_fix1(self, l):
        self.ts(self.ada[:, l, 16:32], self.ada[:, l, 16:32], 1.0, None, ALU.add)

    def ada_fix2(self, l):
        self.ts(self.ada[:, l, 64:80], self.ada[:, l, 64:80], 1.0, None, ALU.add)

    def ada_extras(self, ti, l):
        if ti != 0 or len(self.layers) == 1:
            return []
        l0, l1 = self.layers[0], self.layers[1]
        if l == l0:
            return [(l0, sl) for sl in range(16, 48)] + [(l1, sl) for sl in range(16)]
        return [(l1, sl) for sl in range(16, 48)]

    def flush_silu(self):
        for f in self.pending_silu:
            f()
        self.pending_silu = []

    def ssm_bulk(self, l, g, hoist=False):
        prm = self.prm
        tap_eng = "pool" if hoist else "dve"
        b = g % 2
        xsF = self.xsF
        BFg, CFg = self.BF[b], self.CF[b]
        xsT, BTg, zs = self.xsT[b], self.BT[b], self.zs[b]
        st = {}
        items = []
        blks = [(2 * g, xsF[:, 0, :]), (2 * g + 1, xsF[:, 1, :]), (16 + g, BFg[:, :]), (24 + g, CFg[:, :])]

        def mm_part(j, half):
            def f():
                if hoist and j == 0 and half == 0:
                    st["wv0"] = self.wload(self.w_in[l, 20 + g], KC, 256)
                    st["wv1"] = self.wload(self.w_in[l, 28 + g], KC, 256)
                    st["wz"] = self.wload(self.w_in[l, 12 + g], KC, 256)
                if (not hoist) and j % 2 == 0 and half == 0:
                    st["wv%d" % (j // 2)] = self.wload(self.w_in[l, (20 if j < 2 else 28) + g], KC, 256)
                if half == 0:
                    st["pb%d" % j] = self.psn()
                wv = st["wv%d" % (j // 2)]
                pb = st["pb%d" % j]
                m = j % 2
                for kc in range(half * 8, half * 8 + 8):
                    self.mm(pb[:, :], wv[:, kc, m * 128:(m + 1) * 128], self.hT[:, kc, :], kc == 0, kc == KC - 1)
            return f

        def post(j):
            def f():
                blk, dst = blks[j]
                self.blk_ctr += 1
                q = self.blk_ctr % 2
                raw, acc = self.raw[q], self.cacc[j]
                st["acc%d" % j] = acc
                pb = st["pb%d" % j]
                wof = 8 + blk * 4
                self.cp(raw[:, 0:3], self.ccarry[:, l, blk, :])
                self.act(raw[:, 3:515], pb[:, :], AF.Identity)
                self.act(acc[:, :], pb[:, :], AF.Identity, scale=prm[:, l, wof + 3:wof + 4])
                self.cp(self.ccarry[:, l, blk, :], raw[:, 512:515])
                if tap_eng == "pool":
                    tmp = self.raw[1 - q][:, 0:512]
                    for k in range(3):
                        self.ts(tmp, raw[:, k:k + 512], prm[:, l, wof + k:wof + k + 1], None, ALU.mult, eng="pool")
                        self.tt(acc[:, :], acc[:, :], tmp, ALU.add, eng="pool")
                else:
                    for k in range(3):
                        self.stt(acc[:, :], raw[:, k:k + 512], prm[:, l, wof + k:wof + k + 1], acc[:, :], ALU.mult, ALU.add)
            return f

        def silu(j):
            def f():
                blk, dst = blks[j]
                acc = st["acc%d" % j]
                self.pending_silu.append(lambda: self.act(dst, acc[:, :], AF.Silu, bias=prm[:, l, 136 + blk:137 + blk]))
            return f

        def z_part(c, half):
            def f():
                if (not hoist) and c == 0 and half == 0:
                    st["wz"] = self.wload(self.w_in[l, 12 + g], KC, 256)
                if c % 2 == 0 and half == 0:
                    st["pz%d" % (c // 2)] = self.psn()
                reg = st["pz%d" % (c // 2)][:, (c % 2) * 256:(c % 2) * 256 + 256]
                for kc in range(half * 8, half * 8 + 8):
                    self.mm(reg, self.hT[:, kc, c * 128:(c + 1) * 128], st["wz"][:, kc, :], kc == 0, kc == KC - 1)
            return f

        def z_silu(i2):
            def f():
                self.flush_silu()
                self.act(zs[:, 2 * i2:2 * i2 + 2, :].rearrange("p c f -> p (c f)"), st["pz%d" % i2][:, :], AF.Silu)
            return f

        def t_xs():
            self.flush_silu()
            pbx = self.psn()[:, :].bitcast(BF16)
            for c in range(NCH):
                for j in range(2):
                    self.tr(pbx[:, c * 256 + j * 128:c * 256 + (j + 1) * 128], xsF[:, j, c * 128:(c + 1) * 128],
                            self.ident_b[:, :])
            self.cp(xsT[:, :, :].rearrange("p c f -> p (c f)"), pbx[:, :])

        def t_b():
            self.flush_silu()
            pbb = self.psn()[:, 0:256].bitcast(BF16)
            for c in range(NCH):
                self.tr(pbb[:, c * 128:(c + 1) * 128], BFg[:, c * 128:(c + 1) * 128], self.ident_b[:, :])
            self.cp(BTg[:, :, :].rearrange("p c f -> p (c f)"), pbb[:, 0:512])

        items += [mm_part(0, 0), mm_part(0, 1), post(0), mm_part(1, 0), mm_part(1, 1), post(1), silu(0),
                  mm_part(2, 0), mm_part(2, 1), post(2), silu(1), mm_part(3, 0), mm_part(3, 1), post(3), silu(2),
                  z_part(0, 0), z_part(0, 1), silu(3), z_part(1, 0), z_part(1, 1), z_silu(0), t_xs,
                  z_part(2, 0), z_part(2, 1), t_b, z_part(3, 0), z_part(3, 1), z_silu(1)]
        return items

    def ssm(self, ti, l):
        prm = self.prm
        wv = self.wload(self.w_dt[l], KC, 32)
        pb = self.psn()
        for c in range(NCH):
            for kc in range(KC):
                self.mm(pb[:, c * 32:(c + 1) * 32], self.hT[:, kc, c * 128:(c + 1) * 128], wv[:, kc, 0:32],
                        kc == 0, kc == KC - 1)
        dts = self.dts
        self.tt(dts[:, :, :], pb[:, 0:128].rearrange("p (c h) -> p c h", c=NCH),
                bc(prm[:, l, 184:216], 1, [128, NCH, 32]), ALU.add)
        self.act(dts[:, :, :], dts[:, :, :], AF.Exp)
        self.act(dts[:, :, :], dts[:, :, :], AF.Ln, bias=self.cst[:, 2:3])
        self.tt(self.da[:, :, :], dts[:, :, :], bc(self.abc[:, l, :], 1, [128, NCH, 32]), ALU.mult)
        da2 = self.da[:, :, :].rearrange("p c h -> p (c h)")
        p2 = self.psn()
        self.mm(p2[:, 0:128], self.lmask[:, :], da2, True, True)
        self.mm(p2[:, 128:256], self.umask[:, :], da2, True, True)
        self.mm(p2[:, 256:384], self.ones_f[:, :], da2, True, True)
        self.act(self.e3[:, :, :, :].rearrange("p a c h -> p (a c h)"), p2[:, 0:384], AF.Exp)
        ea = self.e3[:, 0, :, :]
        ds = self.e3[:, 1, :, :]
        cd = self.e3[:, 2, :, :]
        self.tt(self.dtds[:, :, :], dts[:, :, :], ds, ALU.mult)
        self.tap("dts", dts[:, :, :], [128, NCH, 32])
        self.tap("e3", self.e3[:, :, :, :], [128, 3, NCH, 32])

        self.ps_ring = [0, 1, 2, 3]
        self.pending_silu = []
        all_extra = self.ada_extras(ti, l)
        has_extra = len(all_extra) > 0
        per_g = (len(all_extra) + 7) // 8
        for it in self.ssm_bulk(l, 0, hoist=not has_extra):
            it()
        self.flush_silu()
        for g in range(8):
            b = g % 2
            hs = slice(g * 4, g * 4 + 4)
            bulk = self.ssm_bulk(l, g + 1, hoist=not has_extra) if g + 1 < 8 else []
            if has_extra:
                extra = [(lambda la=la, sl=sl: self.ada_slab(la, sl)) for (la, sl) in all_extra[g * per_g:(g + 1) * per_g]]
                merged, k = [], 0
                for idx, it in enumerate(bulk if bulk else [None] * 28):
                    if it is not None:
                        merged.append(it)
                    if idx % 5 == 4 and k < len(extra):
                        merged.append(extra[k])
                        k += 1
                merged += extra[k:]
                bulk = merged
            bulk = list(bulk)

            def fill(n):
                for _ in range(n):
                    if bulk:
                        bulk.pop(0)()

            BFg, CFg = self.BF[b], self.CF[b]
            xsT, BTg, zs = self.xsT[b], self.BT[b], self.zs[b]
            if g == 0:
                self.tap("xsT0", xsT[:, :, :], [128, NCH, 256], BF16)
                self.tap("BT0", BTg[:, :, :], [128, NCH, 128], BF16)
                self.tap("CF0", CFg[:, :], [128, T], BF16)
                self.tap("zs0", zs[:, :, :], [128, NCH, 256], BF16)
            Sg = self.state[:, l, g * 256:(g + 1) * 256]
            Sg3 = Sg.rearrange("p (h q) -> p h q", h=4)
            stbf = self.stbf[b]
            self.cp(stbf[:, :], Sg, eng="act")
            for c in range(NCH):
                q = c % 2
                cs = slice(c * 128, (c + 1) * 128)
                pA, pB, pC, pD = self.ps[4], self.ps[5], self.ps[6], self.ps[7]
                rdal = self.rdal[q]
                self.tt(rdal[:, :, :], bc(self.lmask[:, :], 1, [128, 4, 128]), bc(self.da[:, c, hs], 2, [128, 4, 128]),
                        ALU.mult, eng="pool")
                self.mm(pA[:, 0:128], BFg[:, cs], CFg[:, cs], True, True)
                xs3 = xsT[:, c, :].rearrange("p (h q) -> p h q", h=4)
                xd, xdds = self.xd[q], self.xdds[q]
                self.tt(xdds[:, :, :], xs3, bc(self.dtds[:, c, hs], 2, [128, 4, 64]), ALU.mult, eng="pool")
                self.tt(xd[:, :, :], xs3, bc(self.dts[:, c, hs], 2, [128, 4, 64]), ALU.mult, eng="pool")
                self.mm(pC[:, :], self.umask[:, :], rdal[:, :, :].rearrange("p h t -> p (h t)"), True, True)
                self.mm(pB[:, 0:256], CFg[:, cs], stbf[:, :], True, True)
                self.mm(pB[:, 256:512], BTg[:, c, :], xdds[:, :, :].rearrange("p h q -> p (h q)"), True, True)
                ex = self.ex[q]
                self.act(ex[:, :, :].rearrange("p h t -> p (h t)"), pC[:, :], AF.Exp)
                t1, t2 = self.yt1[q], self.yt2[q]
                self.tt(t1[:, :, :], pB[:, 0:256].rearrange("p (h q) -> p h q", h=4), bc(ea[:, c, hs], 2, [128, 4, 64]),
                        ALU.mult)
                self.tt(Sg3, Sg3, bc(cd[:, c, hs], 2, [128, 4, 64]), ALU.mult)
                self.tt(Sg, Sg, pB[:, 256:512], ALU.add)
                if c < NCH - 1:
                    self.cp(stbf[:, :], Sg, eng="act")
                fill(2)
                cbm = self.cbm[q]
                self.tt(cbm[:, :], pA[:, 0:128], self.lmask[:, :], ALU.mult)
                self.tt(t2[:, :, :], xs3, bc(prm[:, l, 248 + g * 4:252 + g * 4], 2, [128, 4, 64]), ALU.mult)
                self.tt(t1[:, :, :], t1[:, :, :], t2[:, :, :], ALU.add)
                MT = self.MT[q]
                self.tt(MT[:, :, :], ex[:, :, :], bc(cbm[:, :], 1, [128, 4, 128]), ALU.mult)
                fill(2)
                for h in range(4):
                    self.mm(pA[:, 128 + h * 64:128 + (h + 1) * 64], MT[:, h, :], xd[:, h, :], True, True)
                fill(1)
                self.tt(t1[:, :, :], t1[:, :, :], pA[:, 128:384].rearrange("p (h q) -> p h q", h=4), ALU.add)
                yg = self.yg[q]
                self.tt(yg[:, :], t1[:, :, :].rearrange("p h q -> p (h q)"), zs[:, c, :], ALU.mult)
                ss = self.sm[:, q:q + 1]
                self.memset(ss, 0.0, eng="dve")
                self.act(self.junk[:, :], yg[:, :], AF.Square, accum_out=ss)
                rs = self.sm[:, 2 + q:3 + q]
                self.act(rs, ss, AF.Ln, bias=self.cst[:, 0:1], scale=1.0 / 256)
                self.act(rs, rs, AF.Exp, scale=-0.5)
                ynT = self.ynT[q]
                self.act(ynT[:, :], yg[:, :], AF.Identity, scale=rs)
                fill(1)
                for j in range(2):
                    self.tr(pD[:, j * 128:(j + 1) * 128], ynT[:, j * 128:(j + 1) * 128], self.ident_f[:, :])
                fill(1)
                for j in range(2):
                    blk = 2 * g + j
                    self.act(self.yn[:, blk, cs], pD[:, j * 128:(j + 1) * 128], AF.Identity,
                             scale=prm[:, l, 280 + blk:281 + blk])
                if g == 0 and c == 0:
                    self.tap("MT00", MT[:, :, :], [128, 4, 128], BF16)
                    self.tap("yg00", yg[:, :], [128, 256])
            fill(len(bulk))
            self.flush_silu()
        self.ps_ring = list(range(7))
        if has_extra:
            if l == self.layers[0]:
                self.ada_fix2(self.layers[0])
                self.ada_fix1(self.layers[1])
            else:
                self.ada_fix2(self.layers[1])

    def gmlp(self, ti, l):
        prm = self.prm
        hTf = lambda kc: self.hT[:, kc, :]
        for sl in range(4):
            wv = self.wload(self.w_in[l, 8 + sl], KC, 256)
            banks = self.tm(wv, KC, 256)
            for i2 in range(2):
                self.act(self.vg[:, 2 * i2:2 * i2 + 2, sl * 256:(sl + 1) * 256],
                         banks[i2][:, :].rearrange("p (c f) -> p c f", c=2), AF.Gelu)

        def ln_chunk(c):
            st = self.bnst
            for i2 in range(2):
                self.op("dve", lambda e, i2=i2, c=c: e.bn_stats(st[:, i2, 0:6], self.vg[:, c, i2 * 512:(i2 + 1) * 512]),
                        [self.vg[:, c, i2 * 512:(i2 + 1) * 512]], [st[:, i2, 0:6]])
            mv = self.sm[:, 4:6]
            self.op("dve", lambda e: e.bn_aggr(mv, st[:, :, 0:6]), [st[:, :, 0:6]], [mv])
            rs = self.sm[:, 6:7]
            self.act(rs, self.sm[:, 5:6], AF.Sqrt, bias=self.cst[:, 1:2])
            self.recip(rs, rs)
            self.ts(self.vn[:, c, :], self.vg[:, c, :], self.sm[:, 4:5], rs, ALU.subtract, ALU.mult)

        for sl in range(4):
            wv = self.wload(self.w_in[l, 4 + sl], KC, 256)
            banks = self.fm(wv, KC, hTf, 2)
            for j in range(2):
                self.act(self.uF[:, sl * 2 + j, :], banks[j][:, :], AF.Gelu)
            ln_chunk(sl)
        self.tap("vn", self.vn[:, :, :], [128, NCH, 1024], BF16)
        self.tap("uF", self.uF[:, :, :], [128, 8, T], BF16)
        for h in range(8):
            pb = self.psn()
            for c in range(NCH):
                self.mm(pb[:, c * 128:(c + 1) * 128], self.vn[:, c, h * 128:(h + 1) * 128], self.Wt[:, l, h, :], True, True)
            tmp = self.gtmp[h % 2]
            self.stt(tmp[:, :, :], pb[:, :].rearrange("p (c t) -> p c t", c=NCH), prm[:, l, 168 + h:169 + h],
                     bc(self.R[:, l, h, :], 1, [128, NCH, 128]), ALU.mult, ALU.add)
            self.tt(self.gm[:, h, :], tmp[:, :, :].rearrange("p c t -> p (c t)"), self.uF[:, h, :], ALU.mult)

    def pool(self, ti, l):
        prm = self.prm
        hTf = lambda kc: self.hT[:, kc, :]

        def stage1(g):
            wv = self.wload(self.w_in[l, g], KC, 256)
            banks = self.fm(wv, KC, hTf, 2)
            raw = self.praw
            self.cp(raw[:, :, 0:16], self.pcarry[:, l, 2 * g:2 * g + 2, :])
            for j in range(2):
                self.act(raw[:, j, 16:528], banks[j][:, :], AF.Identity)
            self.cp(self.pcarry[:, l, 2 * g:2 * g + 2, :], raw[:, :, 512:528])
            cur = raw
            bufs = [self.pA, self.pB]
            for step in range(g + 1):
                sh = 1 << step
                nx = bufs[step % 2]
                self.tt(nx[:, :, sh:528], cur[:, :, sh:528], cur[:, :, 0:528 - sh], ALU.add)
                cur = nx
            w = 2 << g
            pooled = self.pooled[g % 2]
            self.stt(pooled[:, :, :], cur[:, :, 16:528], 1.0 / w, raw[:, :, 16:528], ALU.mult, ALU.subtract)
            if ti == 0:
                self.tt(self.p16[:, :, :], cur[:, :, 16:32], bc(self.invc[:, g, :], 1, [128, 2, 16]), ALU.mult)
                self.tt(pooled[:, :, 0:16], self.p16[:, :, :], raw[:, :, 16:32], ALU.subtract)

        def stage2(g):
            pooled = self.pooled[g % 2]
            pwg = self.wload(self.pool_w[l, g], 2, 256)
            for m in range(2):
                pb = self.psn()
                for kc in range(2):
                    self.mm(pb[:, :], pwg[:, kc, m * 128:(m + 1) * 128], pooled[:, kc, :], kc == 0, kc == 1)
                self.act(self.mixed[:, 2 * g + m, :], pb[:, :], AF.Identity, scale=prm[:, l, 2 * g + m:2 * g + m + 1])

        stage1(0)
        for g in range(4):
            if g + 1 < 4:
                stage1(g + 1)
            stage2(g)

    def merge(self, ti, l):
        hTf = lambda kc: self.hT[:, kc, :]
        srcs = [(self.w_pool_out, self.mixed, 8), (self.w_gmlp_out, self.gm, 8), (self.w_ssm_out, self.yn, 16)]
        for j in range(8):
            for br in range(3):
                wv = self.wload(self.w_in[l, 36 + br * 8 + j], KC, 256)
                gb = self.fm(wv, KC, hTf, 2)
                gs = self.gsb[br % 2]
                for m in range(2):
                    self.act(gs[:, m, :], gb[m][:, :], AF.Sigmoid)
                W, src, nk = srcs[br]
                wv2 = self.wload(W[l, j], nk, 256)
                yb = self.fm(wv2, nk, lambda kc, src=src: src[:, kc, :], 2)
                tmp = self.mtmp[br % 2]
                for m in range(2):
                    if br == 0:
                        self.tt(self.macc[:, m, :], gs[:, m, :], yb[m][:, :], ALU.mult)
                    elif br == 1:
                        self.tt(tmp[:, m, :], gs[:, m, :], yb[m][:, :], ALU.mult)
                        self.tt(self.macc[:, m, :], self.macc[:, m, :], tmp[:, m, :], ALU.add)
                    else:
                        self.tt(tmp[:, m, :], gs[:, m, :], yb[m][:, :], ALU.mult)
                        self.tt(self.mg[:, 2 * j + m, :], self.macc[:, m, :], tmp[:, m, :], ALU.add)

    def wo(self, ti, l):
        for j in range(8):
            wv = self.wload(self.w_o[l, j], KC, 256)
            banks = self.fm(wv, KC, lambda kc: self.mg[:, kc, :], 2)
            for m in range(2):
                blk = 2 * j + m
                self.stt(self.xT[:, blk, :], banks[m][:, :], self.ada[:, l, 32 + blk:33 + blk], self.xT[:, blk, :],
                         ALU.mult, ALU.add)
                self.stat_acc(blk)
        self.stats_ready = True

    def ffn(self, ti, l):
        hTf = lambda kc: self.hT[:, kc, :]
        for j in range(32):
            wv = self.wload(self.w_up[l, j], KC, 256)
            banks = self.fm(wv, KC, hTf, 2)
            for m in range(2):
                r = self.rtmp[m]
                self.act(r[:, :], banks[m][:, :], AF.Relu)
                self.tt(self.hid[:, 2 * j + m, :], r[:, :], r[:, :], ALU.mult)
        for j in range(8):
            banks = [self.psn(), self.psn()]
            for kq in range(4):
                wv = self.wload(self.w_down[l, j, kq], KC, 256)
                for m in range(2):
                    for kc in range(KC):
                        self.mm(banks[m][:, :], wv[:, kc, m * 128:(m + 1) * 128], self.hid[:, kq * 16 + kc, :],
                                kq == 0 and kc == 0, kq == 3 and kc == KC - 1)
            for m in range(2):
                blk = 2 * j + m
                self.stt(self.xT[:, blk, :], banks[m][:, :], self.ada[:, l, 80 + blk:81 + blk], self.xT[:, blk, :],
                         ALU.mult, ALU.add)
                self.stat_acc(blk)
        self.stats_ready = True


def col(v, n):
    return np.ascontiguousarray(np.asarray(v, np.float32).reshape(n, 128).T)


def prep_inputs(inp):
    f = lambda a: np.ascontiguousarray(np.asarray(a, np.float32))
    sh = {}
    def tile_w(w, nk):
        w = np.asarray(w, np.float32)
        K, Fd = w.shape
        kq = K // (nk * 128)
        a = w.reshape(kq, nk, 128, Fd // 256, 256).transpose(3, 0, 2, 1, 4)
        return np.ascontiguousarray(a).reshape(Fd // 256, kq, 128, nk * 256)

    sh["w_ada_t"] = np.stack([tile_w(inp["w_ada"][l], 16)[:, 0] for l in range(DEPTH)])
    sh["b_ada_col"] = np.stack([col(inp["b_ada"][l], 96) for l in range(DEPTH)])
    perm = []
    for c0, n in ((C_P, 4), (C_U, 4), (C_V, 4), (C_Z, 8), (C_X, 8)):
        perm.extend(range(c0, c0 + n * 256))
    for g in range(8):
        perm.extend(range(C_B + g * 128, C_B + (g + 1) * 128))
        perm.extend(range(C_C + g * 128, C_C + (g + 1) * 128))
    perm.extend(range(C_G, C_G + 3 * 2048))
    perm = np.asarray(perm)
    assert perm.size == 60 * 256
    w_in = np.asarray(inp["w_in"], np.float32)
    sh["w_in_t"] = np.stack([tile_w(w_in[l][:, perm], 16)[:, 0] for l in range(DEPTH)])
    sh["w_dt_t"] = np.stack([np.ascontiguousarray(w_in[l][:, C_DT:C_DT + 32].reshape(16, 128, 32).transpose(1, 0, 2)).reshape(128, 512)
                             for l in range(DEPTH)])
    pw = np.asarray(inp["pool_w"], np.float32)
    sh["pool_w_t"] = np.ascontiguousarray(pw.reshape(DEPTH, 4, 2, 128, 256).transpose(0, 1, 3, 2, 4)).reshape(DEPTH, 4, 128, 512)
    prm = np.zeros((DEPTH, 128, 320), np.float32)
    for l in range(DEPTH):
        o = 0
        prm[l, :, 0:8] = col(inp["pool_scale"][l], 8)
        prm[l, :, 8:136] = np.asarray(inp["conv_w"][l], np.float32).T.reshape(32, 128, 4).transpose(1, 0, 2).reshape(128, 128)
        prm[l, :, 136:168] = col(inp["conv_b"][l], 32)
        prm[l, :, 168:176] = col(inp["gmlp_ln_g"][l], 8)
        prm[l, :, 176:184] = col(inp["gmlp_ln_b"][l], 8)
        prm[l, :, 184:216] = np.broadcast_to(np.asarray(inp["dt_bias"][l], np.float32)[None, :], (128, 32))
        prm[l, :, 216:248] = np.broadcast_to(np.asarray(inp["a_log"][l], np.float32)[None, :], (128, 32))
        prm[l, :, 248:280] = np.broadcast_to(np.asarray(inp["d_skip"][l], np.float32)[None, :], (128, 32))
        prm[l, :, 280:296] = col(inp["ssm_norm"][l], 16)
    sh["prm"] = prm
    sh["gmlp_wsT"] = np.ascontiguousarray(np.asarray(inp["gmlp_ws"], np.float32).transpose(0, 3, 1, 2))
    sh["gmlp_bs_bc"] = np.ascontiguousarray(np.broadcast_to(np.asarray(inp["gmlp_bs"], np.float32)[:, None, :, :], (DEPTH, 128, 8, 128)))
    for k, nk in (("w_pool_out", 8), ("w_gmlp_out", 8), ("w_ssm_out", 16), ("w_o", 16), ("w_up", 16)):
        sh[k + "_t"] = np.stack([tile_w(inp[k][l], nk)[:, 0] for l in range(DEPTH)])
    sh["w_down_t"] = np.stack([tile_w(inp["w_down"][l], 16) for l in range(DEPTH)])
    sh["final_norm_col"] = col(inp["final_norm"], KC)
    x = np.asarray(inp["x"], np.float32)
    c = np.asarray(inp["c"], np.float32)
    per = []
    for b in range(x.shape[0]):
        d = dict(sh)
        d["xT"] = np.ascontiguousarray(x[b].T)
        d["c_col"] = col(c[b], KC)
        per.append(d)
    return per


_CACHE = {}


def kernel(**inputs):
    per = prep_inputs(inputs)
    if "nc" not in _CACHE:
        _CACHE["nc"] = Builder().build()
    nc = _CACHE["nc"]
    res = run_bass_kernel_spmd(nc, per, core_ids=list(range(8)))
    out = np.stack([np.ascontiguousarray(r["outT"].T) for r in res.results], axis=0)
    return out.astype(np.float32)
```

```python
import numpy as np
import concourse.bass as bass
import concourse.mybir as mybir
from concourse.bass_utils import run_bass_kernel_spmd

F32 = mybir.dt.float32
BF16 = mybir.dt.bfloat16
AF = mybir.ActivationFunctionType
ALU = mybir.AluOpType

D = 2048
S = 2048
DEPTH = 2
T = 512
NT = S // T
KC = D // 128
NCH = T // 128
IN_PROJ = 15392
C_P, C_U, C_V, C_Z, C_X, C_B, C_C, C_DT, C_G = 0, 1024, 2048, 3072, 5120, 7168, 8192, 9216, 9248
RMS_EPS = 1e-6
LN_EPS = 1e-5
NWS = 4


PAGE = 64
ESZ = {F32: 4, BF16: 2}


class Op:
    __slots__ = ("eng", "fn", "deps", "dma_sem", "dma_val", "sig", "sigval")


class Sched:
    ENGS = ("pe", "act", "dve", "pool", "sp")

    def __init__(self):
        self.ops = {e: [] for e in self.ENGS}
        self.last_w = {}
        self.readers = {}
        self.dma_sems = {}
        self.same_engine_sync = True
        self.bases = {}
        self.pcache = {}
        self.bank_last = {}

    def pages(self, ap):
        name = ap.tensor.name
        b = self.bases.get(name)
        if b is None:
            assert ap.space not in ("SB", "PSUM"), name
            return ()
        key = (name, ap.ap, ap.offset)
        r = self.pcache.get(key)
        if r is not None:
            return r
        space, base = b
        es = ESZ[ap.dtype]
        dims = ap.ap
        pstride = dims[0][0]
        foff = ap.offset % pstride if pstride > 0 else ap.offset
        free = list(dims[1:])
        if not free:
            free = [(1, 1)]
        lstep, lcnt = free[-1]
        outer = free[:-1]
        starts = [foff]
        for (st, cnt) in outer:
            if st == 0:
                continue
            starts = [s0 + st * i for s0 in starts for i in range(cnt)]
        pg = set()
        span = (lcnt - 1) * lstep + 1
        for s0 in starts:
            lo = (base + s0 * es) // PAGE
            hi = (base + (s0 + span) * es - 1) // PAGE
            for p in range(lo, hi + 1):
                pg.add((space, p))
        r = tuple(pg)
        self.pcache[key] = r
        return r

    def add(self, eng, fn, reads=(), writes=(), dma=None):
        op = Op()
        op.eng = eng
        op.fn = fn
        op.deps = set()
        op.sig = False
        op.sigval = 0
        op.dma_sem = None
        op.dma_val = 0
        if dma is not None:
            n = self.dma_sems.get(dma, 0) + 1
            self.dma_sems[dma] = n
            op.dma_sem = dma
            op.dma_val = 16 * n
        rp = set()
        wp = set()
        for a in reads:
            if a is not None and not isinstance(a, (int, float)):
                rp.update(self.pages(a))
        for a in writes:
            if a is not None:
                wp.update(self.pages(a))
        deps = op.deps
        for p in rp:
            w = self.last_w.get(p)
            if w is not None:
                deps.add(w)
        for p in wp:
            w = self.last_w.get(p)
            if w is not None:
                deps.add(w)
            rd = self.readers.get(p)
            if rd:
                deps.update(rd.values())
        banks = set()
        for (sp_, p) in rp:
            if sp_ == "P":
                banks.add(p * PAGE // 2048)
        for (sp_, p) in wp:
            if sp_ == "P":
                banks.add(p * PAGE // 2048)
        for bk in banks:
            bl = self.bank_last.get(bk)
            if bl is None:
                bl = self.bank_last[bk] = {}
            for e2, o2 in bl.items():
                if e2 != eng:
                    deps.add(o2)
            bl[eng] = op
        deps.discard(op)
        rkey = dma if dma is not None else eng
        for p in rp:
            d = self.readers.get(p)
            if d is None:
                d = self.readers[p] = {}
            d[rkey] = op
        for p in wp:
            self.last_w[p] = op
            self.readers[p] = {}
        self.ops[eng].append(op)
        return op

    def emit(self, nc):
        for e in self.ENGS:
            for op in self.ops[e]:
                for d in op.deps:
                    if d.dma_sem is None:
                        if d.eng == op.eng and (d.eng == "pe" or not self.same_engine_sync):
                            continue
                        d.sig = True
        sems = {}
        for e in self.ENGS:
            sems[e] = nc.alloc_semaphore("sem_" + e)
            c = 0
            for op in self.ops[e]:
                if op.dma_sem is None and op.sig:
                    c += 1
                    op.sigval = c
        dsems = {k: nc.alloc_semaphore("dsem_%d" % i) for i, k in enumerate(self.dma_sems)}
        sched = self

        def run_engine(e, engobj):
            waited = {}
            for op in sched.ops[e]:
                need = {}
                for d in op.deps:
                    if d.dma_sem is not None:
                        key = ("d", d.dma_sem)
                        val = d.dma_val
                    else:
                        if d.eng == e and (e == "pe" or not sched.same_engine_sync):
                            continue
                        key = ("e", d.eng)
                        val = d.sigval
                    if val > need.get(key, 0):
                        need[key] = val
                for key, val in need.items():
                    if waited.get(key, 0) >= val:
                        continue
                    waited[key] = val
                    sem = dsems[key[1]] if key[0] == "d" else sems[key[1]]
                    engobj.wait_ge(sem, val)
                ins = op.fn(engobj)
                if op.dma_sem is not None:
                    ins.then_inc(dsems[op.dma_sem], 16)
                elif op.sig:
                    ins.then_inc(sems[e], 1)
            if e == "sp":
                for k, n in sched.dma_sems.items():
                    engobj.wait_ge(dsems[k], 16 * n)

        with nc.Block() as block:
            @block.tensor
            def _(eng):
                run_engine("pe", eng)

            @block.scalar
            def _(eng):
                run_engine("act", eng)

            @block.vector
            def _(eng):
                run_engine("dve", eng)

            @block.gpsimd
            def _(eng):
                run_engine("pool", eng)

            @block.sync
            def _(eng):
                run_engine("sp", eng)


def bc(ap, axis, shape):
    return ap.unsqueeze(axis).to_broadcast(list(shape))


class Builder:
    def __init__(self, ntiles=NT, layers=(0, 1), final=True, taps=(), stop_after=None):
        self.ntiles = ntiles
        self.layers = layers
        self.final = final
        self.taps = set(taps)
        self.stop_after = stop_after
        self.nc = bass.Bass("TRN2", target_bir_lowering=False)
        self.s = Sched()
        self.ps_i = 0
        self.ws_i = 0
        self.tap_outs = {}
        self.sb_off = 16512 + 0
        self.stopped = False
        self.blk_ctr = 0

    def sb_at(self, name, shape, dt, off):
        n = 1
        for d in shape[1:]:
            n *= d
        nb = n * ESZ[dt]
        assert off % 64 == 0 and off + nb <= 229344, (name, off, nb)
        t = self.nc.alloc_sbuf_tensor_at(name, list(shape), dt, offset=off)
        self.s.bases[t.name] = ("S", off)
        return t, nb

    def sb(self, name, shape, dt=F32):
        t, nb = self.sb_at(name, shape, dt, self.sb_off)
        self.sb_off += (nb + 63) // 64 * 64
        return t

    def din(self, name, shape, dt=F32):
        return self.nc.dram_tensor(name, list(shape), dt, kind="ExternalInput").ap()

    def dout(self, name, shape, dt=F32):
        return self.nc.dram_tensor(name, list(shape), dt, kind="ExternalOutput").ap()

    def psn(self):
        ring = self.ps_ring
        self.ps_i = (self.ps_i + 1) % len(ring)
        return self.ps[ring[self.ps_i]]

    def op(self, eng, fn, reads=(), writes=(), dma=None):
        if self.stopped:
            return
        self.s.add(eng, fn, reads, writes, dma)

    def tap(self, name, ap, shape, dt=F32):
        if name not in self.taps or name in self.tap_outs or self.stopped:
            return
        o = self.dout("dbg_" + name, shape, dt)
        self.tap_outs[name] = (shape, dt)
        self.op("sp", lambda e: e.dma_start(out=o, in_=ap), reads=[ap], dma="tap_" + name)

    def stop(self, name):
        if self.stop_after == name:
            self.stopped = True

    def mm(self, out, lhsT, rhs, start, stop):
        rd = [lhsT, rhs] if start else [lhsT, rhs, out]
        self.op("pe", lambda e: e.matmul(out, lhsT, rhs, start=start, stop=stop), rd, [out])

    def tr(self, out, in_, ident):
        self.op("pe", lambda e: e.transpose(out, in_, ident), [in_, ident], [out])

    def act(self, out, in_, func, bias=None, scale=None, accum_out=None):
        kw = {}
        if bias is not None:
            kw["bias"] = bias
        if scale is not None:
            kw["scale"] = scale
        if accum_out is not None:
            kw["accum_out"] = accum_out
        wr = [out] if accum_out is None else [out, accum_out]
        self.op("act", lambda e: e.activation(out, in_, func, **kw), [in_, bias, scale], wr)

    def tt(self, out, in0, in1, op, eng="dve"):
        self.op(eng, lambda e: e.tensor_tensor(out, in0, in1, op), [in0, in1], [out])

    def ts(self, out, in0, s1, s2, op0, op1=None, eng="dve"):
        if op1 is None:
            self.op(eng, lambda e: e.tensor_scalar(out, in0, s1, None, op0), [in0, s1], [out])
        else:
            self.op(eng, lambda e: e.tensor_scalar(out, in0, s1, s2, op0, op1), [in0, s1, s2], [out])

    def stt(self, out, in0, scalar, in1, op0, op1, eng="dve"):
        self.op(eng, lambda e: e.scalar_tensor_tensor(out, in0, scalar, in1, op0, op1), [in0, scalar, in1], [out])

    def cp(self, out, in_, eng="dve"):
        if eng == "act":
            self.act(out, in_, AF.Identity)
            return
        self.op(eng, lambda e: e.tensor_copy(out, in_), [in_], [out])

    def recip(self, out, in_):
        self.op("dve", lambda e: e.reciprocal(out, in_), [in_], [out])

    def memset(self, ap, val, eng="pool"):
        self.op(eng, lambda e: e.memset(ap, val), [], [ap])

    def dma(self, eng, out, in_, key):
        self.op(eng, lambda e: e.dma_start(out=out, in_=in_), [in_], [out], dma=key)

    def wslot(self):
        i = self.ws_i
        self.ws_i = (i + 1) % NWS
        return i, self.wslots[i]

    def wload(self, src, nk, ncol):
        i, slot = self.wslot()
        self.dma("pool", slot[:, 0:nk * ncol], src, "ws%d" % i)
        return slot[:, 0:nk * ncol].rearrange("p (k c) -> p k c", k=nk)

    def fm(self, wv, nk, rhs, nblk=2):
        banks = []
        for m in range(nblk):
            pb = self.psn()
            for kc in range(nk):
                self.mm(pb[:, :], wv[:, kc, m * 128:(m + 1) * 128], rhs(kc), kc == 0, kc == nk - 1)
            banks.append(pb)
        return banks

    def tm(self, wv, nk, ncol):
        banks = [self.psn(), self.psn()]
        for c in range(NCH):
            reg = banks[c // 2][:, (c % 2) * 256:(c % 2) * 256 + ncol]
            for kc in range(nk):
                self.mm(reg, self.hT[:, kc, c * 128:(c + 1) * 128], wv[:, kc, 0:ncol], kc == 0, kc == nk - 1)
        return banks

    def build(self):
        nc = self.nc
        L2 = DEPTH
        self.xT_d = self.din("xT", [D, S])
        self.c_d = self.din("c_col", [128, KC])
        self.w_ada = self.din("w_ada_t", [L2, 48, 128, 4096])
        self.b_ada = self.din("b_ada_col", [L2, 128, 96])
        self.w_in = self.din("w_in_t", [L2, 60, 128, 4096])
        self.w_dt = self.din("w_dt_t", [L2, 128, KC * 32])
        self.pool_w = self.din("pool_w_t", [L2, 4, 128, 512])
        self.prm_d = self.din("prm", [L2, 128, 320])
        self.wst_d = self.din("gmlp_wsT", [L2, 128, 8, 128])
        self.bsb_d = self.din("gmlp_bs_bc", [L2, 128, 8, 128])
        self.w_pool_out = self.din("w_pool_out_t", [L2, 8, 128, 2048])
        self.w_gmlp_out = self.din("w_gmlp_out_t", [L2, 8, 128, 2048])
        self.w_ssm_out = self.din("w_ssm_out_t", [L2, 8, 128, 4096])
        self.w_o = self.din("w_o_t", [L2, 8, 128, 4096])
        self.w_up = self.din("w_up_t", [L2, 32, 128, 4096])
        self.w_down = self.din("w_down_t", [L2, 8, 4, 128, 4096])
        self.fn_d = self.din("final_norm_col", [128, KC])
        self.out_d = self.dout("outT", [D, S])

        sb = self.sb
        self.xT = sb("xT_sb", [128, KC, T])
        self.hT = sb("hT_sb", [128, KC, T], BF16)
        self.wslots = [sb("ws%d" % i, [128, 4096], BF16) for i in range(NWS)]
        self.ones_f = sb("ones_f", [128, 128])
        self.ident_f = sb("ident_f", [128, 128])
        self.ident_b = sb("ident_b", [128, 128], BF16)
        self.lmask = sb("lmask", [128, 128])
        self.umask = sb("umask", [128, 128])
        self.cst = sb("cst", [128, 8])
        self.sq = [sb("sq%d" % i, [128, T]) for i in range(2)]
        self.rstd = sb("rstd", [128, T])
        self.cact = sb("cact", [128, KC], BF16)
        self.ccol = sb("ccol", [128, KC])
        self.ada = sb("ada", [128, L2, 96])
        self.prm = sb("prm_sb", [128, L2, 320])
        self.fncol = sb("fncol", [128, KC])
        self.Wt = sb("Wt", [128, L2, 8, 128], BF16)
        self.R = sb("Rg", [128, L2, 8, 128])
        self.state = sb("state", [128, L2, 2048])
        self.ccarry = sb("ccarry", [128, L2, 32, 3])
        self.pcarry = sb("pcarry", [128, L2, 8, 16])
        self.invc = sb("invc", [128, 4, 16])
        self.abc = sb("abc", [128, L2, 32])
        self.dts = sb("dts", [128, NCH, 32])
        self.da = sb("da", [128, NCH, 32])
        self.e3 = sb("e3", [128, 3, NCH, 32])
        self.dtds = sb("dtds", [128, NCH, 32])
        self.sm = sb("sm", [128, 16])
        S0 = self.sb_off
        self.S0 = S0
        avail = 229344 - S0
        assert avail >= 84000, avail
        at = lambda name, shape, dt, off: self.sb_at(name, shape, dt, S0 + off)[0]
        self.yn = at("yn", [128, KC, T], BF16, 0)
        self.gm = at("gm", [128, 8, T], BF16, 16384)
        self.mixed = at("mixed", [128, 8, T], BF16, 24576)
        X = 32768
        o = X
        def nxt(name, shape, dt):
            nonlocal o
            t, nb = self.sb_at(name, shape, dt, S0 + o)
            o += (nb + 63) // 64 * 64
            return t
        self.raw = [nxt("raw%d" % i, [128, 516], F32) for i in range(2)]
        self.cacc = [nxt("cacc%d" % i, [128, T], F32) for i in range(4)]
        self.xsF = nxt("xsF", [128, 2, T], BF16)
        self.xsT = [nxt("xsT%d" % i, [128, NCH, 256], BF16) for i in range(2)]
        self.BT = [nxt("BT%d" % i, [128, NCH, 128], BF16) for i in range(2)]
        self.BF = [nxt("BF%d" % i, [128, T], BF16) for i in range(2)]
        self.CF = [nxt("CF%d" % i, [128, T], BF16) for i in range(2)]
        self.zs = [nxt("zs%d" % i, [128, NCH, 256], BF16) for i in range(2)]
        self.cbm = [nxt("cbm%d" % i, [128, 128], F32) for i in range(2)]
        self.rdal = [nxt("rdal%d" % i, [128, 4, 128], F32) for i in range(2)]
        self.ex = [nxt("ex%d" % i, [128, 4, 128], F32) for i in range(2)]
        self.MT = [nxt("MT%d" % i, [128, 4, 128], BF16) for i in range(2)]
        self.xd = [nxt("xd%d" % i, [128, 4, 64], BF16) for i in range(2)]
        self.xdds = [nxt("xdds%d" % i, [128, 4, 64], BF16) for i in range(2)]
        self.yt1 = [nxt("yt1_%d" % i, [128, 4, 64], F32) for i in range(2)]
        self.yt2 = [nxt("yt2_%d" % i, [128, 4, 64], F32) for i in range(2)]
        self.yg = [nxt("yg%d" % i, [128, 256], F32) for i in range(2)]
        self.ynT = [nxt("ynT%d" % i, [128, 256], F32) for i in range(2)]
        self.junk = nxt("junk", [128, 256], BF16)
        self.stbf = [nxt("stbf%d" % i, [128, 256], BF16) for i in range(2)]
        assert o <= avail, (o, avail)
        o = X
        self.uF = nxt("uF", [128, 8, T], BF16)
        self.vg = nxt("vg", [128, NCH, 1024], F32)
        self.vn = nxt("vn", [128, NCH, 1024], BF16)
        self.gtmp = [nxt("gtmp%d" % i, [128, NCH, 128], F32) for i in range(2)]
        self.bnst = nxt("bnst", [128, 2, 8], F32)
        assert o <= avail, (o, avail)
        o = X
        self.praw = nxt("praw", [128, 2, 528], F32)
        self.pA = nxt("pA", [128, 2, 528], F32)
        self.pB = nxt("pB", [128, 2, 528], F32)
        self.pooled = [nxt("pooled%d" % i, [128, 2, T], BF16) for i in range(2)]
        self.p16 = nxt("p16", [128, 2, 16], F32)
        o = X
        self.mg = nxt("mg", [128, KC, T], BF16)
        self.gsb = [nxt("gsb%d" % i, [128, 2, T], F32) for i in range(2)]
        self.macc = nxt("macc", [128, 2, T], F32)
        self.mtmp = [nxt("mtmp%d" % i, [128, 2, T], F32) for i in range(2)]
        assert o <= avail, (o, avail)
        o = X
        self.tmpW = nxt("tmpW", [128, 8, 128], F32)
        self.tmpB = nxt("tmpB", [128, 8, 128], F32)
        self.hid = at("hid", [128, 64, T], BF16, 0)
        self.rtmp = [at("rtmp%d" % i, [128, T], F32, 65536 + i * 2048) for i in range(2)]

        self.ps_ring = list(range(7))
        self.stats_ready = False
        self.ps = [nc.alloc_psum_tensor("ps%d" % i, [128, 512], F32) for i in range(8)]
        for i in range(8):
            self.s.bases[self.ps[i].name] = ("P", i * 2048)

        self.setup()
        for ti in range(self.ntiles):
            self.load_x(ti)
            for l in self.layers:
                self.layer(ti, l)
            if self.final:
                self.final_norm(ti)
            self.store_out(ti)
        self.stopped = False
        with nc.allow_low_precision("bf16 matmul operands by design"):
            self.s.emit(nc)
        return nc

    def setup(self):
        ms = self.memset
        ms(self.ones_f[:, :], 1.0)
        ms(self.cst[:, 0:1], RMS_EPS)
        ms(self.cst[:, 1:2], LN_EPS)
        ms(self.cst[:, 2:3], 1.0)
        ms(self.cst[:, 3:4], 0.0)
        ms(self.lmask[:, :], 1.0)
        self.op("pool", lambda e: e.affine_select(out=self.lmask[:, :], in_=self.lmask[:, :], pattern=[[1, 128]],
                                                  compare_op=ALU.is_ge, fill=0.0, base=0, channel_multiplier=-1),
                [self.lmask[:, :]], [self.lmask[:, :]])
        ms(self.umask[:, :], 1.0)
        self.op("pool", lambda e: e.affine_select(out=self.umask[:, :], in_=self.umask[:, :], pattern=[[-1, 128]],
                                                  compare_op=ALU.is_gt, fill=0.0, base=0, channel_multiplier=1),
                [self.umask[:, :]], [self.umask[:, :]])
        ms(self.ident_f[:, :], 1.0)
        self.op("pool", lambda e: e.affine_select(out=self.ident_f[:, :], in_=self.ident_f[:, :], pattern=[[-1, 128]],
                                                  compare_op=ALU.is_equal, fill=0.0, base=0, channel_multiplier=1),
                [self.ident_f[:, :]], [self.ident_f[:, :]])
        self.cp(self.ident_b[:, :], self.ident_f[:, :])
        ms(self.state[:, :, :], 0.0)
        ms(self.ccarry[:, :, :, :], 0.0)
        ms(self.pcarry[:, :, :, :], 0.0)
        for g in range(4):
            w = 2 << g
            ms(self.invc[:, g, :], 1.0 / w)
            for j in range(w - 1):
                ms(self.invc[:, g, j:j + 1], 1.0 / (j + 1))
        self.dma("sp", self.ccol[:, :], self.c_d, "small1")
        self.dma("sp", self.prm[:, :, :], self.prm_d.rearrange("l p n -> p l n"), "small2")
        self.dma("sp", self.fncol[:, :], self.fn_d, "small3")
        self.dma("sp", self.ada[:, :, :], self.b_ada.rearrange("l p n -> p l n"), "small4")
        self.act(self.cact[:, :], self.ccol[:, :], AF.Silu)
        l0 = self.layers[0]
        for sl in range(16):
            self.ada_slab(l0, sl)
        self.ada_fix1(l0)
        if len(self.layers) == 1:
            for sl in range(16, 48):
                self.ada_slab(l0, sl)
            self.ada_fix2(l0)
        for l in range(DEPTH):
            self.act(self.abc[:, l, :], self.prm[:, l, 216:248], AF.Exp)
            self.ts(self.abc[:, l, :], self.abc[:, l, :], -1.0, None, ALU.mult)
            self.dma("sp", self.tmpW[:, :, :], self.wst_d[l], "small5")
            self.dma("sp", self.tmpB[:, :, :], self.bsb_d[l], "small6")
            self.tt(self.tmpW[:, :, :], self.tmpW[:, :, :], bc(self.lmask[:, :], 1, [128, 8, 128]), ALU.mult)
            self.cp(self.Wt[:, l, :, :], self.tmpW[:, :, :])
            for half in range(2):
                pr = self.psn()
                self.mm(pr[:, :], self.ones_f[:, :],
                        self.tmpW[:, half * 4:(half + 1) * 4, :].rearrange("p h t -> p (h t)"), True, True)
                for hh in range(4):
                    h = half * 4 + hh
                    self.stt(self.R[:, l, h, :], pr[:, hh * 128:(hh + 1) * 128], self.prm[:, l, 176 + h:177 + h],
                             self.tmpB[:, h, :], ALU.mult, ALU.add)
        self.tap("ada", self.ada[:, :, :], [128, DEPTH, 96])
        self.tap("R", self.R[:, :, :, :], [128, DEPTH, 8, 128])

    def load_x(self, ti):
        self.stats_ready = False
        for q in range(4):
            src = self.xT_d[q * 512:(q + 1) * 512, ti * T:(ti + 1) * T].rearrange("(kc p) t -> p kc t", p=128)
            self.dma("sp", self.xT[:, q * 4:(q + 1) * 4, :], src, "xin%d" % q)

    def store_out(self, ti):
        for q in range(4):
            dst = self.out_d[q * 512:(q + 1) * 512, ti * T:(ti + 1) * T].rearrange("(kc p) t -> p kc t", p=128)
            self.dma("sp", dst, self.xT[:, q * 4:(q + 1) * 4, :], "xout%d" % q)

    def stat_acc(self, blk):
        j = blk % 2
        self.act(self.sq[j][:, :], self.xT[:, blk, :], AF.Square)
        self.mm(self.ps[7][:, :], self.ones_f[:, :], self.sq[j][:, :], blk == 0, blk == KC - 1)

    def norm_stats(self):
        if not self.stats_ready:
            for kc in range(KC):
                self.stat_acc(kc)
        self.stats_ready = False
        self.act(self.rstd[:, :], self.ps[7][:, :], AF.Sqrt, bias=self.cst[:, 0:1], scale=1.0 / D)
        self.recip(self.rstd[:, :], self.rstd[:, :])

    def norm_mod(self, l, sc_ofs, sh_ofs):
        self.norm_stats()
        for kc in range(KC):
            j = kc % 2
            self.tt(self.sq[j][:, :], self.xT[:, kc, :], self.rstd[:, :], ALU.mult)
            self.act(self.hT[:, kc, :], self.sq[j][:, :], AF.Identity,
                     bias=self.ada[:, l, sh_ofs + kc:sh_ofs + kc + 1], scale=self.ada[:, l, sc_ofs + kc:sc_ofs + kc + 1])

    def final_norm(self, ti):
        self.tap("ada_end", self.ada[:, :, :], [128, DEPTH, 96])
        self.norm_stats()
        for kc in range(KC):
            j = kc % 2
            self.tt(self.sq[j][:, :], self.xT[:, kc, :], self.rstd[:, :], ALU.mult)
            self.act(self.xT[:, kc, :], self.sq[j][:, :], AF.Identity, scale=self.fncol[:, kc:kc + 1])

    def layer(self, ti, l):
        self.norm_mod(l, 16, 0)
        self.tap("h1", self.hT[:, :, :], [128, KC, T], BF16)
        self.stop("h1")
        self.ssm(ti, l)
        self.tap("yn", self.yn[:, :, :], [128, KC, T], BF16)
        self.stop("ssm")
        self.gmlp(ti, l)
        self.tap("gm", self.gm[:, :, :], [128, 8, T], BF16)
        self.stop("gmlp")
        self.pool(ti, l)
        self.tap("mixed", self.mixed[:, :, :], [128, 8, T], BF16)
        self.stop("pool")
        self.merge(ti, l)
        self.tap("mg", self.mg[:, :, :], [128, KC, T], BF16)
        self.stop("merge")
        self.wo(ti, l)
        self.tap("x1", self.xT[:, :, :], [128, KC, T])
        self.stop("wo")
        self.norm_mod(l, 64, 48)
        self.ffn(ti, l)
        self.tap("x2", self.xT[:, :, :], [128, KC, T])
        self.stop("ffn")

    def ada_slab(self, l, sl):
        wv = self.wload(self.w_ada[l, sl], KC, 256)
        pb = self.psn()
        for mb in range(2):
            for kc in range(KC):
                self.mm(pb[:, mb:mb + 1], wv[:, kc, mb * 128:(mb + 1) * 128], self.cact[:, kc:kc + 1],
                        kc == 0, kc == KC - 1)
        fb = sl * 2
        self.tt(self.ada[:, l, fb:fb + 2], pb[:, 0:2], self.ada[:, l, fb:fb + 2], ALU.add)

    def ada_fix1(self, l):
        self.ts(self.ada[:, l, 16:32], self.ada[:, l, 16:32], 1.0, None, ALU.add)

    def ada_fix2(self, l):
        self.ts(self.ada[:, l, 64:80], self.ada[:, l, 64:80], 1.0, None, ALU.add)

    def ada_extras(self, ti, l):
        if ti != 0 or len(self.layers) == 1:
            return []
        l0, l1 = self.layers[0], self.layers[1]
        if l == l0:
            return [(l0, sl) for sl in range(16, 48)] + [(l1, sl) for sl in range(16)]
        return [(l1, sl) for sl in range(16, 48)]

    def gmlp_u_items(self, l):
        items = []
        st = {}
        for sl in range(4):
            def a(sl=sl):
                st["wv"] = self.wload(self.w_in[l, 4 + sl], KC, 256)
                pb = self.psn()
                st["b0"] = pb
                for kc in range(KC):
                    self.mm(pb[:, :], st["wv"][:, kc, 0:128], self.hT[:, kc, :], kc == 0, kc == KC - 1)

            def b(sl=sl):
                pb = self.psn()
                for kc in range(KC):
                    self.mm(pb[:, :], st["wv"][:, kc, 128:256], self.hT[:, kc, :], kc == 0, kc == KC - 1)
                self.act(self.uF[:, sl * 2, :], st["b0"][:, :], AF.Gelu)
                self.act(self.uF[:, sl * 2 + 1, :], pb[:, :], AF.Gelu)
            items += [a, b]
        self.u_done = True
        return items

    def flush_silu(self):
        for f in self.pending_silu:
            f()
        self.pending_silu = []

    def ssm_bulk(self, l, g, hoist=False):
        prm = self.prm
        tap_eng = "dve"
        b = g % 2
        xsF = self.xsF
        BFg, CFg = self.BF[b], self.CF[b]
        xsT, BTg, zs = self.xsT[b], self.BT[b], self.zs[b]
        st = {}
        items = []
        blks = [(2 * g, xsF[:, 0, :]), (2 * g + 1, xsF[:, 1, :]), (16 + g, BFg[:, :]), (24 + g, CFg[:, :])]

        def mm_part(j, half):
            def f():
                if hoist and j == 0 and half == 0:
                    st["wv0"] = self.wload(self.w_in[l, 20 + g], KC, 256)
                    st["wv1"] = self.wload(self.w_in[l, 28 + g], KC, 256)
                    st["wz"] = self.wload(self.w_in[l, 12 + g], KC, 256)
                if (not hoist) and j % 2 == 0 and half == 0:
                    st["wv%d" % (j // 2)] = self.wload(self.w_in[l, (20 if j < 2 else 28) + g], KC, 256)
                if half == 0:
                    st["pb%d" % j] = self.psn()
                wv = st["wv%d" % (j // 2)]
                pb = st["pb%d" % j]
                m = j % 2
                for kc in range(half * 8, half * 8 + 8):
                    self.mm(pb[:, :], wv[:, kc, m * 128:(m + 1) * 128], self.hT[:, kc, :], kc == 0, kc == KC - 1)
            return f

        def post(j):
            def f():
                blk, dst = blks[j]
                self.blk_ctr += 1
                q = self.blk_ctr % 2
                raw, acc = self.raw[q], self.cacc[j]
                st["acc%d" % j] = acc
                pb = st["pb%d" % j]
                wof = 8 + blk * 4
                self.cp(raw[:, 0:3], self.ccarry[:, l, blk, :])
                self.act(raw[:, 3:515], pb[:, :], AF.Identity)
                self.act(acc[:, :], pb[:, :], AF.Identity, scale=prm[:, l, wof + 3:wof + 4])
                self.cp(self.ccarry[:, l, blk, :], raw[:, 512:515])
                for k in range(3):
                    self.stt(acc[:, :], raw[:, k:k + 512], prm[:, l, wof + k:wof + k + 1], acc[:, :], ALU.mult, ALU.add,
                             eng=tap_eng)
            return f

        def silu(j):
            def f():
                blk, dst = blks[j]
                acc = st["acc%d" % j]
                self.pending_silu.append(lambda: self.act(dst, acc[:, :], AF.Silu, bias=prm[:, l, 136 + blk:137 + blk]))
            return f

        def z_part(c, half):
            def f():
                if (not hoist) and c == 0 and half == 0:
                    st["wz"] = self.wload(self.w_in[l, 12 + g], KC, 256)
                if c % 2 == 0 and half == 0:
                    st["pz%d" % (c // 2)] = self.psn()
                reg = st["pz%d" % (c // 2)][:, (c % 2) * 256:(c % 2) * 256 + 256]
                for kc in range(half * 8, half * 8 + 8):
                    self.mm(reg, self.hT[:, kc, c * 128:(c + 1) * 128], st["wz"][:, kc, :], kc == 0, kc == KC - 1)
            return f

        def z_silu(i2):
            def f():
                self.flush_silu()
                self.act(zs[:, 2 * i2:2 * i2 + 2, :].rearrange("p c f -> p (c f)"), st["pz%d" % i2][:, :], AF.Silu)
            return f

        def t_xs():
            self.flush_silu()
            pbx = self.psn()[:, :].bitcast(BF16)
            for c in range(NCH):
                for j in range(2):
                    self.tr(pbx[:, c * 256 + j * 128:c * 256 + (j + 1) * 128], xsF[:, j, c * 128:(c + 1) * 128],
                            self.ident_b[:, :])
            self.cp(xsT[:, :, :].rearrange("p c f -> p (c f)"), pbx[:, :])

        def t_b():
            self.flush_silu()
            pbb = self.psn()[:, 0:256].bitcast(BF16)
            for c in range(NCH):
                self.tr(pbb[:, c * 128:(c + 1) * 128], BFg[:, c * 128:(c + 1) * 128], self.ident_b[:, :])
            self.cp(BTg[:, :, :].rearrange("p c f -> p (c f)"), pbb[:, 0:512])

        items += [mm_part(0, 0), mm_part(0, 1), post(0), mm_part(1, 0), mm_part(1, 1), post(1), silu(0),
                  mm_part(2, 0), mm_part(2, 1), post(2), silu(1), mm_part(3, 0), mm_part(3, 1), post(3), silu(2),
                  z_part(0, 0), z_part(0, 1), silu(3), z_part(1, 0), z_part(1, 1), z_silu(0), t_xs,
                  z_part(2, 0), z_part(2, 1), t_b, z_part(3, 0), z_part(3, 1), z_silu(1)]
        return items

    def ssm(self, ti, l):
        prm = self.prm
        wv = self.wload(self.w_dt[l], KC, 32)
        pb = self.psn()
        for c in range(NCH):
            for kc in range(KC):
                self.mm(pb[:, c * 32:(c + 1) * 32], self.hT[:, kc, c * 128:(c + 1) * 128], wv[:, kc, 0:32],
                        kc == 0, kc == KC - 1)
        dts = self.dts
        self.tt(dts[:, :, :], pb[:, 0:128].rearrange("p (c h) -> p c h", c=NCH),
                bc(prm[:, l, 184:216], 1, [128, NCH, 32]), ALU.add)
        self.act(dts[:, :, :], dts[:, :, :], AF.Exp)
        self.act(dts[:, :, :], dts[:, :, :], AF.Ln, bias=self.cst[:, 2:3])
        self.tt(self.da[:, :, :], dts[:, :, :], bc(self.abc[:, l, :], 1, [128, NCH, 32]), ALU.mult)
        da2 = self.da[:, :, :].rearrange("p c h -> p (c h)")
        p2 = self.psn()
        self.mm(p2[:, 0:128], self.lmask[:, :], da2, True, True)
        self.mm(p2[:, 128:256], self.umask[:, :], da2, True, True)
        self.mm(p2[:, 256:384], self.ones_f[:, :], da2, True, True)
        self.act(self.e3[:, :, :, :].rearrange("p a c h -> p (a c h)"), p2[:, 0:384], AF.Exp)
        ea = self.e3[:, 0, :, :]
        ds = self.e3[:, 1, :, :]
        cd = self.e3[:, 2, :, :]
        self.tt(self.dtds[:, :, :], dts[:, :, :], ds, ALU.mult)
        self.tap("dts", dts[:, :, :], [128, NCH, 32])
        self.tap("e3", self.e3[:, :, :, :], [128, 3, NCH, 32])

        self.ps_ring = [0, 1, 2, 3]
        self.pending_silu = []
        all_extra = self.ada_extras(ti, l)
        has_extra = len(all_extra) > 0
        per_g = (len(all_extra) + 7) // 8
        for it in self.ssm_bulk(l, 0, hoist=False):
            it()
        self.flush_silu()
        for g in range(8):
            b = g % 2
            hs = slice(g * 4, g * 4 + 4)
            bulk = self.ssm_bulk(l, g + 1, hoist=False) if g + 1 < 8 else self.gmlp_u_items(l)
            if has_extra:
                extra = [(lambda la=la, sl=sl: self.ada_slab(la, sl)) for (la, sl) in all_extra[g * per_g:(g + 1) * per_g]]
                merged, k = [], 0
                for idx, it in enumerate(bulk if bulk else [None] * 28):
                    if it is not None:
                        merged.append(it)
                    if idx % 5 == 4 and k < len(extra):
                        merged.append(extra[k])
                        k += 1
                merged += extra[k:]
                bulk = merged
            bulk = list(bulk)

            def fill(n):
                for _ in range(n):
                    if bulk:
                        bulk.pop(0)()

            BFg, CFg = self.BF[b], self.CF[b]
            xsT, BTg, zs = self.xsT[b], self.BT[b], self.zs[b]
            if g == 0:
                self.tap("xsT0", xsT[:, :, :], [128, NCH, 256], BF16)
                self.tap("BT0", BTg[:, :, :], [128, NCH, 128], BF16)
                self.tap("CF0", CFg[:, :], [128, T], BF16)
                self.tap("zs0", zs[:, :, :], [128, NCH, 256], BF16)
            Sg = self.state[:, l, g * 256:(g + 1) * 256]
            Sg3 = Sg.rearrange("p (h q) -> p h q", h=4)
            stbf = self.stbf[b]
            self.cp(stbf[:, :], Sg, eng="act")
            for c in range(NCH):
                q = c % 2
                cs = slice(c * 128, (c + 1) * 128)
                pA, pB, pC, pD = self.ps[4], self.ps[5], self.ps[6], self.ps[7]
                rdal = self.rdal[q]
                self.tt(rdal[:, :, :], bc(self.lmask[:, :], 1, [128, 4, 128]), bc(self.da[:, c, hs], 2, [128, 4, 128]),
                        ALU.mult, eng="pool")
                self.mm(pA[:, 0:128], BFg[:, cs], CFg[:, cs], True, True)
                xs3 = xsT[:, c, :].rearrange("p (h q) -> p h q", h=4)
                xd, xdds = self.xd[q], self.xdds[q]
                self.tt(xdds[:, :, :], xs3, bc(self.dtds[:, c, hs], 2, [128, 4, 64]), ALU.mult, eng="pool")
                self.tt(xd[:, :, :], xs3, bc(self.dts[:, c, hs], 2, [128, 4, 64]), ALU.mult, eng="pool")
                self.mm(pC[:, :], self.umask[:, :], rdal[:, :, :].rearrange("p h t -> p (h t)"), True, True)
                self.mm(pB[:, 0:256], CFg[:, cs], stbf[:, :], True, True)
                self.mm(pB[:, 256:512], BTg[:, c, :], xdds[:, :, :].rearrange("p h q -> p (h q)"), True, True)
                ex = self.ex[q]
                self.act(ex[:, :, :].rearrange("p h t -> p (h t)"), pC[:, :], AF.Exp)
                t1, t2 = self.yt1[q], self.yt2[q]
                self.tt(t1[:, :, :], pB[:, 0:256].rearrange("p (h q) -> p h q", h=4), bc(ea[:, c, hs], 2, [128, 4, 64]),
                        ALU.mult)
                self.tt(Sg3, Sg3, bc(cd[:, c, hs], 2, [128, 4, 64]), ALU.mult)
                self.tt(Sg, Sg, pB[:, 256:512], ALU.add)
                if c < NCH - 1:
                    self.cp(stbf[:, :], Sg, eng="act")
                fill(2)
                cbm = self.cbm[q]
                self.tt(cbm[:, :], pA[:, 0:128], self.lmask[:, :], ALU.mult)
                self.tt(t2[:, :, :], xs3, bc(prm[:, l, 248 + g * 4:252 + g * 4], 2, [128, 4, 64]), ALU.mult)
                self.tt(t1[:, :, :], t1[:, :, :], t2[:, :, :], ALU.add)
                MT = self.MT[q]
                self.tt(MT[:, :, :], ex[:, :, :], bc(cbm[:, :], 1, [128, 4, 128]), ALU.mult)
                fill(2)
                for h in range(4):
                    self.mm(pA[:, 128 + h * 64:128 + (h + 1) * 64], MT[:, h, :], xd[:, h, :], True, True)
                fill(1)
                self.tt(t1[:, :, :], t1[:, :, :], pA[:, 128:384].rearrange("p (h q) -> p h q", h=4), ALU.add)
                yg = self.yg[q]
                self.tt(yg[:, :], t1[:, :, :].rearrange("p h q -> p (h q)"), zs[:, c, :], ALU.mult)
                ss = self.sm[:, q:q + 1]
                self.memset(ss, 0.0, eng="dve")
                self.act(self.junk[:, :], yg[:, :], AF.Square, accum_out=ss)
                rs = self.sm[:, 2 + q:3 + q]
                self.act(rs, ss, AF.Ln, bias=self.cst[:, 0:1], scale=1.0 / 256)
                self.act(rs, rs, AF.Exp, scale=-0.5)
                ynT = self.ynT[q]
                self.act(ynT[:, :], yg[:, :], AF.Identity, scale=rs)
                fill(1)
                for j in range(2):
                    self.tr(pD[:, j * 128:(j + 1) * 128], ynT[:, j * 128:(j + 1) * 128], self.ident_f[:, :])
                fill(1)
                for j in range(2):
                    blk = 2 * g + j
                    self.act(self.yn[:, blk, cs], pD[:, j * 128:(j + 1) * 128], AF.Identity,
                             scale=prm[:, l, 280 + blk:281 + blk])
                if g == 0 and c == 0:
                    self.tap("MT00", MT[:, :, :], [128, 4, 128], BF16)
                    self.tap("yg00", yg[:, :], [128, 256])
            fill(len(bulk))
            self.flush_silu()
        self.ps_ring = list(range(7))
        if has_extra:
            if l == self.layers[0]:
                self.ada_fix2(self.layers[0])
                self.ada_fix1(self.layers[1])
            else:
                self.ada_fix2(self.layers[1])

    def gmlp(self, ti, l):
        prm = self.prm
        hTf = lambda kc: self.hT[:, kc, :]
        for sl in range(4):
            wv = self.wload(self.w_in[l, 8 + sl], KC, 256)
            banks = self.tm(wv, KC, 256)
            for i2 in range(2):
                self.act(self.vg[:, 2 * i2:2 * i2 + 2, sl * 256:(sl + 1) * 256],
                         banks[i2][:, :].rearrange("p (c f) -> p c f", c=2), AF.Gelu)

        def ln_chunk(c):
            st = self.bnst
            for i2 in range(2):
                self.op("dve", lambda e, i2=i2, c=c: e.bn_stats(st[:, i2, 0:6], self.vg[:, c, i2 * 512:(i2 + 1) * 512]),
                        [self.vg[:, c, i2 * 512:(i2 + 1) * 512]], [st[:, i2, 0:6]])
            mv = self.sm[:, 4:6]
            self.op("dve", lambda e: e.bn_aggr(mv, st[:, :, 0:6]), [st[:, :, 0:6]], [mv])
            rs = self.sm[:, 6:7]
            self.act(rs, self.sm[:, 5:6], AF.Sqrt, bias=self.cst[:, 1:2])
            self.recip(rs, rs)
            self.ts(self.vn[:, c, :], self.vg[:, c, :], self.sm[:, 4:5], rs, ALU.subtract, ALU.mult)

        if getattr(self, "u_done", False):
            for c in range(NCH):
                ln_chunk(c)
        else:
            for sl in range(4):
                wv = self.wload(self.w_in[l, 4 + sl], KC, 256)
                banks = self.fm(wv, KC, hTf, 2)
                for j in range(2):
                    self.act(self.uF[:, sl * 2 + j, :], banks[j][:, :], AF.Gelu)
                ln_chunk(sl)
        self.u_done = False
        self.tap("vn", self.vn[:, :, :], [128, NCH, 1024], BF16)
        self.tap("uF", self.uF[:, :, :], [128, 8, T], BF16)
        for h in range(8):
            pb = self.psn()
            for c in range(NCH):
                self.mm(pb[:, c * 128:(c + 1) * 128], self.vn[:, c, h * 128:(h + 1) * 128], self.Wt[:, l, h, :], True, True)
            tmp = self.gtmp[h % 2]
            self.stt(tmp[:, :, :], pb[:, :].rearrange("p (c t) -> p c t", c=NCH), prm[:, l, 168 + h:169 + h],
                     bc(self.R[:, l, h, :], 1, [128, NCH, 128]), ALU.mult, ALU.add)
            self.tt(self.gm[:, h, :], tmp[:, :, :].rearrange("p c t -> p (c t)"), self.uF[:, h, :], ALU.mult)

    def pool(self, ti, l):
        prm = self.prm
        hTf = lambda kc: self.hT[:, kc, :]

        def stage1(g):
            wv = self.wload(self.w_in[l, g], KC, 256)
            banks = self.fm(wv, KC, hTf, 2)
            raw = self.praw
            self.cp(raw[:, :, 0:16], self.pcarry[:, l, 2 * g:2 * g + 2, :])
            for j in range(2):
                self.act(raw[:, j, 16:528], banks[j][:, :], AF.Identity)
            self.cp(self.pcarry[:, l, 2 * g:2 * g + 2, :], raw[:, :, 512:528])
            cur = raw
            bufs = [self.pA, self.pB]
            for step in range(g + 1):
                sh = 1 << step
                nx = bufs[step % 2]
                self.tt(nx[:, :, sh:528], cur[:, :, sh:528], cur[:, :, 0:528 - sh], ALU.add)
                cur = nx
            w = 2 << g
            pooled = self.pooled[g % 2]
            self.stt(pooled[:, :, :], cur[:, :, 16:528], 1.0 / w, raw[:, :, 16:528], ALU.mult, ALU.subtract)
            if ti == 0:
                self.tt(self.p16[:, :, :], cur[:, :, 16:32], bc(self.invc[:, g, :], 1, [128, 2, 16]), ALU.mult)
                self.tt(pooled[:, :, 0:16], self.p16[:, :, :], raw[:, :, 16:32], ALU.subtract)

        def stage2(g):
            pooled = self.pooled[g % 2]
            pwg = self.wload(self.pool_w[l, g], 2, 256)
            for m in range(2):
                pb = self.psn()
                for kc in range(2):
                    self.mm(pb[:, :], pwg[:, kc, m * 128:(m + 1) * 128], pooled[:, kc, :], kc == 0, kc == 1)
                self.act(self.mixed[:, 2 * g + m, :], pb[:, :], AF.Identity, scale=prm[:, l, 2 * g + m:2 * g + m + 1])

        stage1(0)
        for g in range(4):
            if g + 1 < 4:
                stage1(g + 1)
            stage2(g)

    def merge(self, ti, l):
        hTf = lambda kc: self.hT[:, kc, :]
        srcs = [(self.w_pool_out, self.mixed, 8), (self.w_gmlp_out, self.gm, 8), (self.w_ssm_out, self.yn, 16)]
        for j in range(8):
            for br in range(3):
                wv = self.wload(self.w_in[l, 36 + br * 8 + j], KC, 256)
                gb = self.fm(wv, KC, hTf, 2)
                gs = self.gsb[br % 2]
                for m in range(2):
                    self.act(gs[:, m, :], gb[m][:, :], AF.Sigmoid)
                W, src, nk = srcs[br]
                wv2 = self.wload(W[l, j], nk, 256)
                yb = self.fm(wv2, nk, lambda kc, src=src: src[:, kc, :], 2)
                tmp = self.mtmp[br % 2]
                for m in range(2):
                    if br == 0:
                        self.tt(self.macc[:, m, :], gs[:, m, :], yb[m][:, :], ALU.mult)
                    elif br == 1:
                        self.tt(tmp[:, m, :], gs[:, m, :], yb[m][:, :], ALU.mult)
                        self.tt(self.macc[:, m, :], self.macc[:, m, :], tmp[:, m, :], ALU.add)
                    else:
                        self.tt(tmp[:, m, :], gs[:, m, :], yb[m][:, :], ALU.mult)
                        self.tt(self.mg[:, 2 * j + m, :], self.macc[:, m, :], tmp[:, m, :], ALU.add)

    def wo(self, ti, l):
        for j in range(8):
            wv = self.wload(self.w_o[l, j], KC, 256)
            banks = self.fm(wv, KC, lambda kc: self.mg[:, kc, :], 2)
            for m in range(2):
                blk = 2 * j + m
                self.stt(self.xT[:, blk, :], banks[m][:, :], self.ada[:, l, 32 + blk:33 + blk], self.xT[:, blk, :],
                         ALU.mult, ALU.add)
                self.stat_acc(blk)
        self.stats_ready = True

    def ffn(self, ti, l):
        hTf = lambda kc: self.hT[:, kc, :]
        for j in range(32):
            wv = self.wload(self.w_up[l, j], KC, 256)
            banks = self.fm(wv, KC, hTf, 2)
            for m in range(2):
                r = self.rtmp[m]
                self.act(r[:, :], banks[m][:, :], AF.Relu)
                self.tt(self.hid[:, 2 * j + m, :], r[:, :], r[:, :], ALU.mult)
        for j in range(8):
            banks = [self.psn(), self.psn()]
            for kq in range(4):
                wv = self.wload(self.w_down[l, j, kq], KC, 256)
                for m in range(2):
                    for kc in range(KC):
                        self.mm(banks[m][:, :], wv[:, kc, m * 128:(m + 1) * 128], self.hid[:, kq * 16 + kc, :],
                                kq == 0 and kc == 0, kq == 3 and kc == KC - 1)
            for m in range(2):
                blk = 2 * j + m
                self.stt(self.xT[:, blk, :], banks[m][:, :], self.ada[:, l, 80 + blk:81 + blk], self.xT[:, blk, :],
                         ALU.mult, ALU.add)
                self.stat_acc(blk)
        self.stats_ready = True


def col(v, n):
    return np.ascontiguousarray(np.asarray(v, np.float32).reshape(n, 128).T)


def prep_inputs(inp):
    f = lambda a: np.ascontiguousarray(np.asarray(a, np.float32))
    sh = {}
    def tile_w(w, nk):
        w = np.asarray(w, np.float32)
        K, Fd = w.shape
        kq = K // (nk * 128)
        a = w.reshape(kq, nk, 128, Fd // 256, 256).transpose(3, 0, 2, 1, 4)
        return np.ascontiguousarray(a).reshape(Fd // 256, kq, 128, nk * 256)

    sh["w_ada_t"] = np.stack([tile_w(inp["w_ada"][l], 16)[:, 0] for l in range(DEPTH)])
    sh["b_ada_col"] = np.stack([col(inp["b_ada"][l], 96) for l in range(DEPTH)])
    perm = []
    for c0, n in ((C_P, 4), (C_U, 4), (C_V, 4), (C_Z, 8), (C_X, 8)):
        perm.extend(range(c0, c0 + n * 256))
    for g in range(8):
        perm.extend(range(C_B + g * 128, C_B + (g + 1) * 128))
        perm.extend(range(C_C + g * 128, C_C + (g + 1) * 128))
    perm.extend(range(C_G, C_G + 3 * 2048))
    perm = np.asarray(perm)
    assert perm.size == 60 * 256
    w_in = np.asarray(inp["w_in"], np.float32)
    sh["w_in_t"] = np.stack([tile_w(w_in[l][:, perm], 16)[:, 0] for l in range(DEPTH)])
    sh["w_dt_t"] = np.stack([np.ascontiguousarray(w_in[l][:, C_DT:C_DT + 32].reshape(16, 128, 32).transpose(1, 0, 2)).reshape(128, 512)
                             for l in range(DEPTH)])
    pw = np.asarray(inp["pool_w"], np.float32)
    sh["pool_w_t"] = np.ascontiguousarray(pw.reshape(DEPTH, 4, 2, 128, 256).transpose(0, 1, 3, 2, 4)).reshape(DEPTH, 4, 128, 512)
    prm = np.zeros((DEPTH, 128, 320), np.float32)
    for l in range(DEPTH):
        o = 0
        prm[l, :, 0:8] = col(inp["pool_scale"][l], 8)
        prm[l, :, 8:136] = np.asarray(inp["conv_w"][l], np.float32).T.reshape(32, 128, 4).transpose(1, 0, 2).reshape(128, 128)
        prm[l, :, 136:168] = col(inp["conv_b"][l], 32)
        prm[l, :, 168:176] = col(inp["gmlp_ln_g"][l], 8)
        prm[l, :, 176:184] = col(inp["gmlp_ln_b"][l], 8)
        prm[l, :, 184:216] = np.broadcast_to(np.asarray(inp["dt_bias"][l], np.float32)[None, :], (128, 32))
        prm[l, :, 216:248] = np.broadcast_to(np.asarray(inp["a_log"][l], np.float32)[None, :], (128, 32))
        prm[l, :, 248:280] = np.broadcast_to(np.asarray(inp["d_skip"][l], np.float32)[None, :], (128, 32))
        prm[l, :, 280:296] = col(inp["ssm_norm"][l], 16)
    sh["prm"] = prm
    sh["gmlp_wsT"] = np.ascontiguousarray(np.asarray(inp["gmlp_ws"], np.float32).transpose(0, 3, 1, 2))
    sh["gmlp_bs_bc"] = np.ascontiguousarray(np.broadcast_to(np.asarray(inp["gmlp_bs"], np.float32)[:, None, :, :], (DEPTH, 128, 8, 128)))
    for k, nk in (("w_pool_out", 8), ("w_gmlp_out", 8), ("w_ssm_out", 16), ("w_o", 16), ("w_up", 16)):
        sh[k + "_t"] = np.stack([tile_w(inp[k][l], nk)[:, 0] for l in range(DEPTH)])
    sh["w_down_t"] = np.stack([tile_w(inp["w_down"][l], 16) for l in range(DEPTH)])
    sh["final_norm_col"] = col(inp["final_norm"], KC)
    x = np.asarray(inp["x"], np.float32)
    c = np.asarray(inp["c"], np.float32)
    per = []
    for b in range(x.shape[0]):
        d = dict(sh)
        d["xT"] = np.ascontiguousarray(x[b].T)
        d["c_col"] = col(c[b], KC)
        per.append(d)
    return per


_CACHE = {}


def kernel(**inputs):
    per = prep_inputs(inputs)
    if "nc" not in _CACHE:
        _CACHE["nc"] = Builder().build()
    nc = _CACHE["nc"]
    res = run_bass_kernel_spmd(nc, per, core_ids=list(range(8)))
    out = np.stack([np.ascontiguousarray(r["outT"].T) for r in res.results], axis=0)
    return out.astype(np.float32)
```

```python
import numpy as np
import concourse.bass as bass
import concourse.mybir as mybir
from concourse.bass_utils import run_bass_kernel_spmd

F32 = mybir.dt.float32
BF16 = mybir.dt.bfloat16
AF = mybir.ActivationFunctionType
ALU = mybir.AluOpType

D = 2048
S = 2048
DEPTH = 2
T = 512
NT = S // T
KC = D // 128
NCH = T // 128
IN_PROJ = 15392
C_P, C_U, C_V, C_Z, C_X, C_B, C_C, C_DT, C_G = 0, 1024, 2048, 3072, 5120, 7168, 8192, 9216, 9248
RMS_EPS = 1e-6
LN_EPS = 1e-5
NWS = 4


PAGE = 64
ESZ = {F32: 4, BF16: 2}


class Op:
    __slots__ = ("eng", "fn", "deps", "dma_sem", "dma_val", "sig", "sigval")


class Sched:
    ENGS = ("pe", "act", "dve", "pool", "sp")

    def __init__(self):
        self.ops = {e: [] for e in self.ENGS}
        self.last_w = {}
        self.readers = {}
        self.dma_sems = {}
        self.same_engine_sync = True
        self.bases = {}
        self.pcache = {}
        self.bank_last = {}

    def pages(self, ap):
        name = ap.tensor.name
        b = self.bases.get(name)
        if b is None:
            assert ap.space not in ("SB", "PSUM"), name
            return ()
        key = (name, ap.ap, ap.offset)
        r = self.pcache.get(key)
        if r is not None:
            return r
        space, base = b
        es = ESZ[ap.dtype]
        dims = ap.ap
        pstride = dims[0][0]
        foff = ap.offset % pstride if pstride > 0 else ap.offset
        free = list(dims[1:])
        if not free:
            free = [(1, 1)]
        lstep, lcnt = free[-1]
        outer = free[:-1]
        starts = [foff]
        for (st, cnt) in outer:
            if st == 0:
                continue
            starts = [s0 + st * i for s0 in starts for i in range(cnt)]
        pg = set()
        span = (lcnt - 1) * lstep + 1
        for s0 in starts:
            lo = (base + s0 * es) // PAGE
            hi = (base + (s0 + span) * es - 1) // PAGE
            for p in range(lo, hi + 1):
                pg.add((space, p))
        r = tuple(pg)
        self.pcache[key] = r
        return r

    def add(self, eng, fn, reads=(), writes=(), dma=None):
        op = Op()
        op.eng = eng
        op.fn = fn
        op.deps = set()
        op.sig = False
        op.sigval = 0
        op.dma_sem = None
        op.dma_val = 0
        if dma is not None:
            n = self.dma_sems.get(dma, 0) + 1
            self.dma_sems[dma] = n
            op.dma_sem = dma
            op.dma_val = 16 * n
        rp = set()
        wp = set()
        for a in reads:
            if a is not None and not isinstance(a, (int, float)):
                rp.update(self.pages(a))
        for a in writes:
            if a is not None:
                wp.update(self.pages(a))
        deps = op.deps
        for p in rp:
            w = self.last_w.get(p)
            if w is not None:
                deps.add(w)
        for p in wp:
            w = self.last_w.get(p)
            if w is not None:
                deps.add(w)
            rd = self.readers.get(p)
            if rd:
                deps.update(rd.values())
        banks = set()
        for (sp_, p) in rp:
            if sp_ == "P":
                banks.add(p * PAGE // 2048)
        for (sp_, p) in wp:
            if sp_ == "P":
                banks.add(p * PAGE // 2048)
        for bk in banks:
            bl = self.bank_last.get(bk)
            if bl is None:
                bl = self.bank_last[bk] = {}
            for e2, o2 in bl.items():
                if e2 != eng:
                    deps.add(o2)
            bl[eng] = op
        deps.discard(op)
        rkey = dma if dma is not None else eng
        for p in rp:
            d = self.readers.get(p)
            if d is None:
                d = self.readers[p] = {}
            d[rkey] = op
        for p in wp:
            self.last_w[p] = op
            self.readers[p] = {}
        self.ops[eng].append(op)
        return op

    def emit(self, nc):
        for e in self.ENGS:
            for op in self.ops[e]:
                for d in op.deps:
                    if d.dma_sem is None:
                        if d.eng == op.eng and (d.eng == "pe" or not self.same_engine_sync):
                            continue
                        d.sig = True
        sems = {}
        for e in self.ENGS:
            sems[e] = nc.alloc_semaphore("sem_" + e)
            c = 0
            for op in self.ops[e]:
                if op.dma_sem is None and op.sig:
                    c += 1
                    op.sigval = c
        dsems = {k: nc.alloc_semaphore("dsem_%d" % i) for i, k in enumerate(self.dma_sems)}
        sched = self

        def run_engine(e, engobj):
            waited = {}
            for op in sched.ops[e]:
                need = {}
                for d in op.deps:
                    if d.dma_sem is not None:
                        key = ("d", d.dma_sem)
                        val = d.dma_val
                    else:
                        if d.eng == e and (e == "pe" or not sched.same_engine_sync):
                            continue
                        key = ("e", d.eng)
                        val = d.sigval
                    if val > need.get(key, 0):
                        need[key] = val
                for key, val in need.items():
                    if waited.get(key, 0) >= val:
                        continue
                    waited[key] = val
                    sem = dsems[key[1]] if key[0] == "d" else sems[key[1]]
                    engobj.wait_ge(sem, val)
                ins = op.fn(engobj)
                if op.dma_sem is not None:
                    ins.then_inc(dsems[op.dma_sem], 16)
                elif op.sig:
                    ins.then_inc(sems[e], 1)
            if e == "sp":
                for k, n in sched.dma_sems.items():
                    engobj.wait_ge(dsems[k], 16 * n)

        with nc.Block() as block:
            @block.tensor
            def _(eng):
                run_engine("pe", eng)

            @block.scalar
            def _(eng):
                run_engine("act", eng)

            @block.vector
            def _(eng):
                run_engine("dve", eng)

            @block.gpsimd
            def _(eng):
                run_engine("pool", eng)

            @block.sync
            def _(eng):
                run_engine("sp", eng)


def bc(ap, axis, shape):
    return ap.unsqueeze(axis).to_broadcast(list(shape))


class Builder:
    def __init__(self, ntiles=NT, layers=(0, 1), final=True, taps=(), stop_after=None):
        self.ntiles = ntiles
        self.layers = layers
        self.final = final
        self.taps = set(taps)
        self.stop_after = stop_after
        self.nc = bass.Bass("TRN2", target_bir_lowering=False)
        self.s = Sched()
        self.ps_i = 0
        self.ws_i = 0
        self.tap_outs = {}
        self.sb_off = 16512 + 0
        self.stopped = False
        self.blk_ctr = 0

    def sb_at(self, name, shape, dt, off):
        n = 1
        for d in shape[1:]:
            n *= d
        nb = n * ESZ[dt]
        assert off % 64 == 0 and off + nb <= 229344, (name, off, nb)
        t = self.nc.alloc_sbuf_tensor_at(name, list(shape), dt, offset=off)
        self.s.bases[t.name] = ("S", off)
        return t, nb

    def sb(self, name, shape, dt=F32):
        t, nb = self.sb_at(name, shape, dt, self.sb_off)
        self.sb_off += (nb + 63) // 64 * 64
        return t

    def din(self, name, shape, dt=F32):
        return self.nc.dram_tensor(name, list(shape), dt, kind="ExternalInput").ap()

    def dout(self, name, shape, dt=F32):
        return self.nc.dram_tensor(name, list(shape), dt, kind="ExternalOutput").ap()

    def psn(self):
        ring = self.ps_ring
        self.ps_i = (self.ps_i + 1) % len(ring)
        return self.ps[ring[self.ps_i]]

    def op(self, eng, fn, reads=(), writes=(), dma=None):
        if self.stopped:
            return
        self.s.add(eng, fn, reads, writes, dma)

    def tap(self, name, ap, shape, dt=F32):
        if name not in self.taps or name in self.tap_outs or self.stopped:
            return
        o = self.dout("dbg_" + name, shape, dt)
        self.tap_outs[name] = (shape, dt)
        self.op("sp", lambda e: e.dma_start(out=o, in_=ap), reads=[ap], dma="tap_" + name)

    def stop(self, name):
        if self.stop_after == name:
            self.stopped = True

    def mm(self, out, lhsT, rhs, start, stop):
        rd = [lhsT, rhs] if start else [lhsT, rhs, out]
        self.op("pe", lambda e: e.matmul(out, lhsT, rhs, start=start, stop=stop), rd, [out])

    def tr(self, out, in_, ident):
        self.op("pe", lambda e: e.transpose(out, in_, ident), [in_, ident], [out])

    def act(self, out, in_, func, bias=None, scale=None, accum_out=None):
        kw = {}
        if bias is not None:
            kw["bias"] = bias
        if scale is not None:
            kw["scale"] = scale
        if accum_out is not None:
            kw["accum_out"] = accum_out
        wr = [out] if accum_out is None else [out, accum_out]
        self.op("act", lambda e: e.activation(out, in_, func, **kw), [in_, bias, scale], wr)

    def tt(self, out, in0, in1, op, eng="dve"):
        self.op(eng, lambda e: e.tensor_tensor(out, in0, in1, op), [in0, in1], [out])

    def ts(self, out, in0, s1, s2, op0, op1=None, eng="dve"):
        if op1 is None:
            self.op(eng, lambda e: e.tensor_scalar(out, in0, s1, None, op0), [in0, s1], [out])
        else:
            self.op(eng, lambda e: e.tensor_scalar(out, in0, s1, s2, op0, op1), [in0, s1, s2], [out])

    def stt(self, out, in0, scalar, in1, op0, op1, eng="dve"):
        self.op(eng, lambda e: e.scalar_tensor_tensor(out, in0, scalar, in1, op0, op1), [in0, scalar, in1], [out])

    def cp(self, out, in_, eng="dve"):
        if eng == "act":
            self.act(out, in_, AF.Identity)
            return
        self.op(eng, lambda e: e.tensor_copy(out, in_), [in_], [out])

    def recip(self, out, in_):
        self.op("dve", lambda e: e.reciprocal(out, in_), [in_], [out])

    def memset(self, ap, val, eng="pool"):
        self.op(eng, lambda e: e.memset(ap, val), [], [ap])

    def dma(self, eng, out, in_, key):
        self.op(eng, lambda e: e.dma_start(out=out, in_=in_), [in_], [out], dma=key)

    def wslot(self):
        i = self.ws_i
        self.ws_i = (i + 1) % NWS
        return i, self.wslots[i]

    def wload(self, src, nk, ncol):
        i, slot = self.wslot()
        self.dma("pool", slot[:, 0:nk * ncol], src, "ws%d" % i)
        return slot[:, 0:nk * ncol].rearrange("p (k c) -> p k c", k=nk)

    def fm(self, wv, nk, rhs, nblk=2):
        banks = []
        for m in range(nblk):
            pb = self.psn()
            for kc in range(nk):
                self.mm(pb[:, :], wv[:, kc, m * 128:(m + 1) * 128], rhs(kc), kc == 0, kc == nk - 1)
            banks.append(pb)
        return banks

    def tm(self, wv, nk, ncol):
        banks = [self.psn(), self.psn()]
        for c in range(NCH):
            reg = banks[c // 2][:, (c % 2) * 256:(c % 2) * 256 + ncol]
            for kc in range(nk):
                self.mm(reg, self.hT[:, kc, c * 128:(c + 1) * 128], wv[:, kc, 0:ncol], kc == 0, kc == nk - 1)
        return banks

    def build(self):
        nc = self.nc
        L2 = DEPTH
        self.xT_d = self.din("xT", [D, S])
        self.c_d = self.din("c_col", [128, KC])
        self.w_ada = self.din("w_ada_t", [L2, 48, 128, 4096])
        self.b_ada = self.din("b_ada_col", [L2, 128, 96])
        self.w_in = self.din("w_in_t", [L2, 60, 128, 4096])
        self.w_dt = self.din("w_dt_t", [L2, 128, KC * 32])
        self.pool_w = self.din("pool_w_t", [L2, 4, 128, 512])
        self.prm_d = self.din("prm", [L2, 128, 320])
        self.wst_d = self.din("gmlp_wsT", [L2, 128, 8, 128])
        self.bsb_d = self.din("gmlp_bs_bc", [L2, 128, 8, 128])
        self.w_pool_out = self.din("w_pool_out_t", [L2, 8, 128, 2048])
        self.w_gmlp_out = self.din("w_gmlp_out_t", [L2, 8, 128, 2048])
        self.w_ssm_out = self.din("w_ssm_out_t", [L2, 8, 128, 4096])
        self.w_o = self.din("w_o_t", [L2, 8, 128, 4096])
        self.w_up = self.din("w_up_t", [L2, 32, 128, 4096])
        self.w_down = self.din("w_down_t", [L2, 8, 4, 128, 4096])
        self.fn_d = self.din("final_norm_col", [128, KC])
        self.out_d = self.dout("outT", [D, S])

        sb = self.sb
        self.xT = sb("xT_sb", [128, KC, T])
        self.hT = sb("hT_sb", [128, KC, T], BF16)
        self.wslots = [sb("ws%d" % i, [128, 4096], BF16) for i in range(NWS)]
        self.ones_f = sb("ones_f", [128, 128])
        self.ident_f = sb("ident_f", [128, 128])
        self.ident_b = sb("ident_b", [128, 128], BF16)
        self.lmask = sb("lmask", [128, 128])
        self.umask = sb("umask", [128, 128])
        self.cst = sb("cst", [128, 8])
        self.sq = [sb("sq%d" % i, [128, T]) for i in range(2)]
        self.rstd = sb("rstd", [128, T])
        self.cact = sb("cact", [128, KC], BF16)
        self.ccol = sb("ccol", [128, KC])
        self.ada = sb("ada", [128, L2, 96])
        self.prm = sb("prm_sb", [128, L2, 320])
        self.fncol = sb("fncol", [128, KC])
        self.Wt = sb("Wt", [128, L2, 8, 128], BF16)
        self.R = sb("Rg", [128, L2, 8, 128])
        self.state = sb("state", [128, L2, 2048])
        self.ccarry = sb("ccarry", [128, L2, 32, 3])
        self.pcarry = sb("pcarry", [128, L2, 8, 16])
        self.invc = sb("invc", [128, 4, 16])
        self.abc = sb("abc", [128, L2, 32])
        self.dts = sb("dts", [128, NCH, 32])
        self.da = sb("da", [128, NCH, 32])
        self.e3 = sb("e3", [128, 3, NCH, 32])
        self.dtds = sb("dtds", [128, NCH, 32])
        self.sm = sb("sm", [128, 16])
        S0 = self.sb_off
        self.S0 = S0
        avail = 229344 - S0
        assert avail >= 84000, avail
        at = lambda name, shape, dt, off: self.sb_at(name, shape, dt, S0 + off)[0]
        self.yn = at("yn", [128, KC, T], BF16, 0)
        self.gm = at("gm", [128, 8, T], BF16, 16384)
        self.mixed = at("mixed", [128, 8, T], BF16, 24576)
        X = 32768
        o = X
        def nxt(name, shape, dt):
            nonlocal o
            t, nb = self.sb_at(name, shape, dt, S0 + o)
            o += (nb + 63) // 64 * 64
            return t
        self.raw = [nxt("raw%d" % i, [128, 516], F32) for i in range(2)]
        self.cacc = [nxt("cacc%d" % i, [128, T], F32) for i in range(4)]
        self.xsF = nxt("xsF", [128, 2, T], BF16)
        self.xsT = [nxt("xsT%d" % i, [128, NCH, 256], BF16) for i in range(2)]
        self.BT = [nxt("BT%d" % i, [128, NCH, 128], BF16) for i in range(2)]
        self.BF = [nxt("BF%d" % i, [128, T], BF16) for i in range(2)]
        self.CF = [nxt("CF%d" % i, [128, T], BF16) for i in range(2)]
        self.zs = [nxt("zs%d" % i, [128, NCH, 256], BF16) for i in range(2)]
        self.cbm = [nxt("cbm%d" % i, [128, 128], F32) for i in range(2)]
        self.rdal = [nxt("rdal%d" % i, [128, 4, 128], F32) for i in range(2)]
        self.ex = [nxt("ex%d" % i, [128, 4, 128], F32) for i in range(2)]
        self.MT = [nxt("MT%d" % i, [128, 4, 128], BF16) for i in range(2)]
        self.xd = [nxt("xd%d" % i, [128, 4, 64], BF16) for i in range(2)]
        self.xdds = [nxt("xdds%d" % i, [128, 4, 64], BF16) for i in range(2)]
        self.yt1 = [nxt("yt1_%d" % i, [128, 4, 64], F32) for i in range(2)]
        self.yt2 = [nxt("yt2_%d" % i, [128, 4, 64], F32) for i in range(2)]
        self.yg = [nxt("yg%d" % i, [128, 256], F32) for i in range(2)]
        self.ynT = [nxt("ynT%d" % i, [128, 256], F32) for i in range(2)]
        self.junk = nxt("junk", [128, 256], BF16)
        self.stbf = [nxt("stbf%d" % i, [128, 256], BF16) for i in range(2)]
        assert o <= avail, (o, avail)
        o = X
        self.uF = nxt("uF", [128, 8, T], BF16)
        self.vg = nxt("vg", [128, NCH, 1024], F32)
        self.vn = nxt("vn", [128, NCH, 1024], BF16)
        self.gtmp = [nxt("gtmp%d" % i, [128, NCH, 128], F32) for i in range(2)]
        self.bnst = nxt("bnst", [128, 2, 8], F32)
        assert o <= avail, (o, avail)
        o = X
        self.praw = nxt("praw", [128, 2, 528], F32)
        self.pA = nxt("pA", [128, 2, 528], F32)
        self.pB = nxt("pB", [128, 2, 528], F32)
        self.pooled = [nxt("pooled%d" % i, [128, 2, T], BF16) for i in range(2)]
        self.p16 = nxt("p16", [128, 2, 16], F32)
        o = X
        self.mg = nxt("mg", [128, KC, T], BF16)
        self.gsb = [nxt("gsb%d" % i, [128, 2, T], F32) for i in range(2)]
        self.macc = nxt("macc", [128, 2, T], F32)
        self.mtmp = [nxt("mtmp%d" % i, [128, 2, T], F32) for i in range(2)]
        assert o <= avail, (o, avail)
        o = X
        self.tmpW = nxt("tmpW", [128, 8, 128], F32)
        self.tmpB = nxt("tmpB", [128, 8, 128], F32)
        self.hid = at("hid", [128, 64, T], BF16, 0)
        self.rtmp = [at("rtmp%d" % i, [128, T], F32, 65536 + i * 2048) for i in range(2)]

        self.ps_ring = list(range(7))
        self.stats_ready = False
        self.ps = [nc.alloc_psum_tensor("ps%d" % i, [128, 512], F32) for i in range(8)]
        for i in range(8):
            self.s.bases[self.ps[i].name] = ("P", i * 2048)

        self.setup()
        for ti in range(self.ntiles):
            self.load_x(ti)
            for l in self.layers:
                self.layer(ti, l)
            if self.final:
                self.final_norm(ti)
            self.store_out(ti)
        self.stopped = False
        with nc.allow_low_precision("bf16 matmul operands by design"):
            self.s.emit(nc)
        return nc

    def setup(self):
        ms = self.memset
        ms(self.ones_f[:, :], 1.0)
        ms(self.cst[:, 0:1], RMS_EPS)
        ms(self.cst[:, 1:2], LN_EPS)
        ms(self.cst[:, 2:3], 1.0)
        ms(self.cst[:, 3:4], 0.0)
        ms(self.lmask[:, :], 1.0)
        self.op("pool", lambda e: e.affine_select(out=self.lmask[:, :], in_=self.lmask[:, :], pattern=[[1, 128]],
                                                  compare_op=ALU.is_ge, fill=0.0, base=0, channel_multiplier=-1),
                [self.lmask[:, :]], [self.lmask[:, :]])
        ms(self.umask[:, :], 1.0)
        self.op("pool", lambda e: e.affine_select(out=self.umask[:, :], in_=self.umask[:, :], pattern=[[-1, 128]],
                                                  compare_op=ALU.is_gt, fill=0.0, base=0, channel_multiplier=1),
                [self.umask[:, :]], [self.umask[:, :]])
        ms(self.ident_f[:, :], 1.0)
        self.op("pool", lambda e: e.affine_select(out=self.ident_f[:, :], in_=self.ident_f[:, :], pattern=[[-1, 128]],
                                                  compare_op=ALU.is_equal, fill=0.0, base=0, channel_multiplier=1),
                [self.ident_f[:, :]], [self.ident_f[:, :]])
        self.cp(self.ident_b[:, :], self.ident_f[:, :])
        ms(self.state[:, :, :], 0.0)
        ms(self.ccarry[:, :, :, :], 0.0)
        ms(self.pcarry[:, :, :, :], 0.0)
        for g in range(4):
            w = 2 << g
            ms(self.invc[:, g, :], 1.0 / w)
            for j in range(w - 1):
                ms(self.invc[:, g, j:j + 1], 1.0 / (j + 1))
        self.dma("sp", self.ccol[:, :], self.c_d, "small1")
        self.dma("sp", self.prm[:, :, :], self.prm_d.rearrange("l p n -> p l n"), "small2")
        self.dma("sp", self.fncol[:, :], self.fn_d, "small3")
        self.dma("sp", self.ada[:, :, :], self.b_ada.rearrange("l p n -> p l n"), "small4")
        self.act(self.cact[:, :], self.ccol[:, :], AF.Silu)
        l0 = self.layers[0]
        for sl in range(16):
            self.ada_slab(l0, sl)
        self.ada_fix1(l0)
        if len(self.layers) == 1:
            for sl in range(16, 48):
                self.ada_slab(l0, sl)
            self.ada_fix2(l0)
        for l in range(DEPTH):
            self.act(self.abc[:, l, :], self.prm[:, l, 216:248], AF.Exp)
            self.ts(self.abc[:, l, :], self.abc[:, l, :], -1.0, None, ALU.mult)
            self.dma("sp", self.tmpW[:, :, :], self.wst_d[l], "small5")
            self.dma("sp", self.tmpB[:, :, :], self.bsb_d[l], "small6")
            self.tt(self.tmpW[:, :, :], self.tmpW[:, :, :], bc(self.lmask[:, :], 1, [128, 8, 128]), ALU.mult)
            self.cp(self.Wt[:, l, :, :], self.tmpW[:, :, :])
            for half in range(2):
                pr = self.psn()
                self.mm(pr[:, :], self.ones_f[:, :],
                        self.tmpW[:, half * 4:(half + 1) * 4, :].rearrange("p h t -> p (h t)"), True, True)
                for hh in range(4):
                    h = half * 4 + hh
                    self.stt(self.R[:, l, h, :], pr[:, hh * 128:(hh + 1) * 128], self.prm[:, l, 176 + h:177 + h],
                             self.tmpB[:, h, :], ALU.mult, ALU.add)
        self.tap("ada", self.ada[:, :, :], [128, DEPTH, 96])
        self.tap("R", self.R[:, :, :, :], [128, DEPTH, 8, 128])

    def load_x(self, ti):
        self.stats_ready = False
        for q in range(4):
            src = self.xT_d[q * 512:(q + 1) * 512, ti * T:(ti + 1) * T].rearrange("(kc p) t -> p kc t", p=128)
            self.dma("sp", self.xT[:, q * 4:(q + 1) * 4, :], src, "xin%d" % q)

    def store_out(self, ti):
        for q in range(4):
            dst = self.out_d[q * 512:(q + 1) * 512, ti * T:(ti + 1) * T].rearrange("(kc p) t -> p kc t", p=128)
            self.dma("sp", dst, self.xT[:, q * 4:(q + 1) * 4, :], "xout%d" % q)

    def stat_acc(self, blk):
        j = blk % 2
        self.act(self.sq[j][:, :], self.xT[:, blk, :], AF.Square)
        self.mm(self.ps[7][:, :], self.ones_f[:, :], self.sq[j][:, :], blk == 0, blk == KC - 1)

    def norm_stats(self):
        if not self.stats_ready:
            for kc in range(KC):
                self.stat_acc(kc)
        self.stats_ready = False
        self.act(self.rstd[:, :], self.ps[7][:, :], AF.Sqrt, bias=self.cst[:, 0:1], scale=1.0 / D)
        self.recip(self.rstd[:, :], self.rstd[:, :])

    def norm_mod(self, l, sc_ofs, sh_ofs):
        self.norm_stats()
        for kc in range(KC):
            j = kc % 2
            self.tt(self.sq[j][:, :], self.xT[:, kc, :], self.rstd[:, :], ALU.mult)
            self.act(self.hT[:, kc, :], self.sq[j][:, :], AF.Identity,
                     bias=self.ada[:, l, sh_ofs + kc:sh_ofs + kc + 1], scale=self.ada[:, l, sc_ofs + kc:sc_ofs + kc + 1])

    def final_norm(self, ti):
        self.tap("ada_end", self.ada[:, :, :], [128, DEPTH, 96])
        self.norm_stats()
        for kc in range(KC):
            j = kc % 2
            self.tt(self.sq[j][:, :], self.xT[:, kc, :], self.rstd[:, :], ALU.mult)
            self.act(self.xT[:, kc, :], self.sq[j][:, :], AF.Identity, scale=self.fncol[:, kc:kc + 1])

    def layer(self, ti, l):
        self.norm_mod(l, 16, 0)
        self.tap("h1", self.hT[:, :, :], [128, KC, T], BF16)
        self.stop("h1")
        self.ssm(ti, l)
        self.tap("yn", self.yn[:, :, :], [128, KC, T], BF16)
        self.stop("ssm")
        self.gmlp(ti, l)
        self.tap("gm", self.gm[:, :, :], [128, 8, T], BF16)
        self.stop("gmlp")
        self.pool(ti, l)
        self.tap("mixed", self.mixed[:, :, :], [128, 8, T], BF16)
        self.stop("pool")
        self.merge(ti, l)
        self.tap("mg", self.mg[:, :, :], [128, KC, T], BF16)
        self.stop("merge")
        self.wo(ti, l)
        self.tap("x1", self.xT[:, :, :], [128, KC, T])
        self.stop("wo")
        self.norm_mod(l, 64, 48)
        self.ffn(ti, l)
        self.tap("x2", self.xT[:, :, :], [128, KC, T])
        self.stop("ffn")

    def ada_slab(self, l, sl):
        wv = self.wload(self.w_ada[l, sl], KC, 256)
        pb = self.psn()
        for mb in range(2):
            for kc in range(KC):
                self.mm(pb[:, mb:mb + 1], wv[:, kc, mb * 128:(mb + 1) * 128], self.cact[:, kc:kc + 1],
                        kc == 0, kc == KC - 1)
        fb = sl * 2
        self.tt(self.ada[:, l, fb:fb + 2], pb[:, 0:2], self.ada[:, l, fb:fb + 2], ALU.add)

    def ada_fix1(self, l):
        self.ts(self.ada[:, l, 16:32], self.ada[:, l, 16:32], 1.0, None, ALU.add)

    def ada_fix2(self, l):
        self.ts(self.ada[:, l, 64:80], self.ada[:, l, 64:80], 1.0, None, ALU.add)

    def ada_extras(self, ti, l):
        if ti != 0 or len(self.layers) == 1:
            return []
        l0, l1 = self.layers[0], self.layers[1]
        if l == l0:
            return [(l0, sl) for sl in range(16, 48)] + [(l1, sl) for sl in range(16)]
        return [(l1, sl) for sl in range(16, 48)]

    def gmlp_u_items(self, l):
        items = []
        st = {}
        for sl in range(4):
            def a(sl=sl):
                st["wv"] = self.wload(self.w_in[l, 4 + sl], KC, 256)
                pb = self.psn()
                st["b0"] = pb
                for kc in range(KC):
                    self.mm(pb[:, :], st["wv"][:, kc, 0:128], self.hT[:, kc, :], kc == 0, kc == KC - 1)

            def b(sl=sl):
                pb = self.psn()
                for kc in range(KC):
                    self.mm(pb[:, :], st["wv"][:, kc, 128:256], self.hT[:, kc, :], kc == 0, kc == KC - 1)
                self.act(self.uF[:, sl * 2, :], st["b0"][:, :], AF.Gelu)
                self.act(self.uF[:, sl * 2 + 1, :], pb[:, :], AF.Gelu)
            items += [a, b]
        self.u_done = True
        return items

    def flush_silu(self):
        for f in self.pending_silu:
            f()
        self.pending_silu = []

    def ssm_bulk(self, l, g, hoist=False):
        prm = self.prm
        tap_eng = "dve"
        b = g % 2
        xsF = self.xsF
        BFg, CFg = self.BF[b], self.CF[b]
        xsT, BTg, zs = self.xsT[b], self.BT[b], self.zs[b]
        st = {}
        items = []
        blks = [(2 * g, xsF[:, 0, :]), (2 * g + 1, xsF[:, 1, :]), (16 + g, BFg[:, :]), (24 + g, CFg[:, :])]

        def mm_part(j, half):
            def f():
                if hoist and j == 0 and half == 0:
                    st["wv0"] = self.wload(self.w_in[l, 20 + g], KC, 256)
                    st["wv1"] = self.wload(self.w_in[l, 28 + g], KC, 256)
                    st["wz"] = self.wload(self.w_in[l, 12 + g], KC, 256)
                if (not hoist) and j % 2 == 0 and half == 0:
                    st["wv%d" % (j // 2)] = self.wload(self.w_in[l, (20 if j < 2 else 28) + g], KC, 256)
                if half == 0:
                    st["pb%d" % j] = self.psn()
                wv = st["wv%d" % (j // 2)]
                pb = st["pb%d" % j]
                m = j % 2
                for kc in range(half * 8, half * 8 + 8):
                    self.mm(pb[:, :], wv[:, kc, m * 128:(m + 1) * 128], self.hT[:, kc, :], kc == 0, kc == KC - 1)
            return f

        def post(j):
            def f():
                blk, dst = blks[j]
                self.blk_ctr += 1
                q = self.blk_ctr % 2
                raw, acc = self.raw[q], self.cacc[j]
                st["acc%d" % j] = acc
                pb = st["pb%d" % j]
                wof = 8 + blk * 4
                self.cp(raw[:, 0:3], self.ccarry[:, l, blk, :])
                self.act(raw[:, 3:515], pb[:, :], AF.Identity)
                self.act(acc[:, :], pb[:, :], AF.Identity, scale=prm[:, l, wof + 3:wof + 4])
                self.cp(self.ccarry[:, l, blk, :], raw[:, 512:515])
                for k in range(3):
                    self.stt(acc[:, :], raw[:, k:k + 512], prm[:, l, wof + k:wof + k + 1], acc[:, :], ALU.mult, ALU.add,
                             eng=tap_eng)
            return f

        def silu(j):
            def f():
                blk, dst = blks[j]
                acc = st["acc%d" % j]
                self.pending_silu.append(lambda: self.act(dst, acc[:, :], AF.Silu, bias=prm[:, l, 136 + blk:137 + blk]))
            return f

        def z_part(c, half):
            def f():
                if (not hoist) and c == 0 and half == 0:
                    st["wz"] = self.wload(self.w_in[l, 12 + g], KC, 256)
                if c % 2 == 0 and half == 0:
                    st["pz%d" % (c // 2)] = self.psn()
                reg = st["pz%d" % (c // 2)][:, (c % 2) * 256:(c % 2) * 256 + 256]
                for kc in range(half * 8, half * 8 + 8):
                    self.mm(reg, self.hT[:, kc, c * 128:(c + 1) * 128], st["wz"][:, kc, :], kc == 0, kc == KC - 1)
            return f

        def z_silu(i2):
            def f():
                self.flush_silu()
                self.act(zs[:, 2 * i2:2 * i2 + 2, :].rearrange("p c f -> p (c f)"), st["pz%d" % i2][:, :], AF.Silu)
            return f

        def t_xs():
            self.flush_silu()
            pbx = self.psn()[:, :].bitcast(BF16)
            for c in range(NCH):
                for j in range(2):
                    self.tr(pbx[:, c * 256 + j * 128:c * 256 + (j + 1) * 128], xsF[:, j, c * 128:(c + 1) * 128],
                            self.ident_b[:, :])
            self.cp(xsT[:, :, :].rearrange("p c f -> p (c f)"), pbx[:, :])

        def t_b():
            self.flush_silu()
            pbb = self.psn()[:, 0:256].bitcast(BF16)
            for c in range(NCH):
                self.tr(pbb[:, c * 128:(c + 1) * 128], BFg[:, c * 128:(c + 1) * 128], self.ident_b[:, :])
            self.cp(BTg[:, :, :].rearrange("p c f -> p (c f)"), pbb[:, 0:512])

        items += [mm_part(0, 0), mm_part(0, 1), post(0), mm_part(1, 0), mm_part(1, 1), post(1), silu(0),
                  mm_part(2, 0), mm_part(2, 1), post(2), silu(1), mm_part(3, 0), mm_part(3, 1), post(3), silu(2),
                  z_part(0, 0), z_part(0, 1), silu(3), z_part(1, 0), z_part(1, 1), z_silu(0), t_xs,
                  z_part(2, 0), z_part(2, 1), t_b, z_part(3, 0), z_part(3, 1), z_silu(1)]
        return items

    def ssm(self, ti, l):
        prm = self.prm
        wv = self.wload(self.w_dt[l], KC, 32)
        pb = self.psn()
        for c in range(NCH):
            for kc in range(KC):
                self.mm(pb[:, c * 32:(c + 1) * 32], self.hT[:, kc, c * 128:(c + 1) * 128], wv[:, kc, 0:32],
                        kc == 0, kc == KC - 1)
        dts = self.dts
        self.tt(dts[:, :, :], pb[:, 0:128].rearrange("p (c h) -> p c h", c=NCH),
                bc(prm[:, l, 184:216], 1, [128, NCH, 32]), ALU.add)
        self.act(dts[:, :, :], dts[:, :, :], AF.Exp)
        self.act(dts[:, :, :], dts[:, :, :], AF.Ln, bias=self.cst[:, 2:3])
        self.tt(self.da[:, :, :], dts[:, :, :], bc(self.abc[:, l, :], 1, [128, NCH, 32]), ALU.mult)
        da2 = self.da[:, :, :].rearrange("p c h -> p (c h)")
        p2 = self.psn()
        self.mm(p2[:, 0:128], self.lmask[:, :], da2, True, True)
        self.mm(p2[:, 128:256], self.umask[:, :], da2, True, True)
        self.mm(p2[:, 256:384], self.ones_f[:, :], da2, True, True)
        self.act(self.e3[:, :, :, :].rearrange("p a c h -> p (a c h)"), p2[:, 0:384], AF.Exp)
        ea = self.e3[:, 0, :, :]
        ds = self.e3[:, 1, :, :]
        cd = self.e3[:, 2, :, :]
        self.tt(self.dtds[:, :, :], dts[:, :, :], ds, ALU.mult)
        self.tap("dts", dts[:, :, :], [128, NCH, 32])
        self.tap("e3", self.e3[:, :, :, :], [128, 3, NCH, 32])

        self.ps_ring = [0, 1, 2, 3]
        self.pending_silu = []
        all_extra = self.ada_extras(ti, l)
        has_extra = len(all_extra) > 0
        per_g = (len(all_extra) + 7) // 8
        for it in self.ssm_bulk(l, 0, hoist=False):
            it()
        self.flush_silu()
        for g in range(8):
            b = g % 2
            hs = slice(g * 4, g * 4 + 4)
            bulk = self.ssm_bulk(l, g + 1, hoist=False) if g + 1 < 8 else self.gmlp_u_items(l)
            if has_extra:
                extra = [(lambda la=la, sl=sl: self.ada_slab(la, sl)) for (la, sl) in all_extra[g * per_g:(g + 1) * per_g]]
                merged, k = [], 0
                for idx, it in enumerate(bulk if bulk else [None] * 28):
                    if it is not None:
                        merged.append(it)
                    if idx % 5 == 4 and k < len(extra):
                        merged.append(extra[k])
                        k += 1
                merged += extra[k:]
                bulk = merged
            bulk = list(bulk)

            def fill(n):
                for _ in range(n):
                    if bulk:
                        bulk.pop(0)()

            BFg, CFg = self.BF[b], self.CF[b]
            xsT, BTg, zs = self.xsT[b], self.BT[b], self.zs[b]
            if g == 0:
                self.tap("xsT0", xsT[:, :, :], [128, NCH, 256], BF16)
                self.tap("BT0", BTg[:, :, :], [128, NCH, 128], BF16)
                self.tap("CF0", CFg[:, :], [128, T], BF16)
                self.tap("zs0", zs[:, :, :], [128, NCH, 256], BF16)
            Sg = self.state[:, l, g * 256:(g + 1) * 256]
            Sg3 = Sg.rearrange("p (h q) -> p h q", h=4)
            stbf = self.stbf[b]
            self.cp(stbf[:, :], Sg, eng="act")
            for c in range(NCH):
                q = c % 2
                cs = slice(c * 128, (c + 1) * 128)
                pA, pB, pC, pD = self.ps[4], self.ps[5], self.ps[6], self.ps[7]
                rdal = self.rdal[q]
                self.tt(rdal[:, :, :], bc(self.lmask[:, :], 1, [128, 4, 128]), bc(self.da[:, c, hs], 2, [128, 4, 128]),
                        ALU.mult, eng="pool")
                self.mm(pA[:, 0:128], BFg[:, cs], CFg[:, cs], True, True)
                xs3 = xsT[:, c, :].rearrange("p (h q) -> p h q", h=4)
                xd, xdds = self.xd[q], self.xdds[q]
                self.tt(xdds[:, :, :], xs3, bc(self.dtds[:, c, hs], 2, [128, 4, 64]), ALU.mult, eng="pool")
                self.tt(xd[:, :, :], xs3, bc(self.dts[:, c, hs], 2, [128, 4, 64]), ALU.mult, eng="pool")
                self.mm(pC[:, :], self.umask[:, :], rdal[:, :, :].rearrange("p h t -> p (h t)"), True, True)
                self.mm(pB[:, 0:256], CFg[:, cs], stbf[:, :], True, True)
                self.mm(pB[:, 256:512], BTg[:, c, :], xdds[:, :, :].rearrange("p h q -> p (h q)"), True, True)
                ex = self.ex[q]
                self.act(ex[:, :, :].rearrange("p h t -> p (h t)"), pC[:, :], AF.Exp)
                t1, t2 = self.yt1[q], self.yt2[q]
                self.tt(t1[:, :, :], pB[:, 0:256].rearrange("p (h q) -> p h q", h=4), bc(ea[:, c, hs], 2, [128, 4, 64]),
                        ALU.mult)
                self.tt(Sg3, Sg3, bc(cd[:, c, hs], 2, [128, 4, 64]), ALU.mult)
                self.tt(Sg, Sg, pB[:, 256:512], ALU.add)
                if c < NCH - 1:
                    self.cp(stbf[:, :], Sg, eng="act")
                fill(2)
                cbm = self.cbm[q]
                self.tt(cbm[:, :], pA[:, 0:128], self.lmask[:, :], ALU.mult)
                self.tt(t2[:, :, :], xs3, bc(prm[:, l, 248 + g * 4:252 + g * 4], 2, [128, 4, 64]), ALU.mult, eng="pool")
                MT = self.MT[q]
                self.tt(MT[:, :, :], ex[:, :, :], bc(cbm[:, :], 1, [128, 4, 128]), ALU.mult)
                self.tt(t1[:, :, :], t1[:, :, :], t2[:, :, :], ALU.add)
                for h in range(4):
                    self.mm(pA[:, 128 + h * 64:128 + (h + 1) * 64], MT[:, h, :], xd[:, h, :], True, True)
                fill(1)
                self.tt(t1[:, :, :], t1[:, :, :], pA[:, 128:384].rearrange("p (h q) -> p h q", h=4), ALU.add)
                yg = self.yg[q]
                self.tt(yg[:, :], t1[:, :, :].rearrange("p h q -> p (h q)"), zs[:, c, :], ALU.mult)
                ss = self.sm[:, q:q + 1]
                self.memset(ss, 0.0, eng="dve")
                self.act(self.junk[:, :], yg[:, :], AF.Square, accum_out=ss)
                rs = self.sm[:, 2 + q:3 + q]
                self.act(rs, ss, AF.Ln, bias=self.cst[:, 0:1], scale=1.0 / 256)
                self.act(rs, rs, AF.Exp, scale=-0.5)
                ynT = self.ynT[q]
                self.act(ynT[:, :], yg[:, :], AF.Identity, scale=rs)
                fill(3)
                for j in range(2):
                    self.tr(pD[:, j * 128:(j + 1) * 128], ynT[:, j * 128:(j + 1) * 128], self.ident_f[:, :])
                fill(1)
                for j in range(2):
                    blk = 2 * g + j
                    self.act(self.yn[:, blk, cs], pD[:, j * 128:(j + 1) * 128], AF.Identity,
                             scale=prm[:, l, 280 + blk:281 + blk])
                if g == 0 and c == 0:
                    self.tap("MT00", MT[:, :, :], [128, 4, 128], BF16)
                    self.tap("yg00", yg[:, :], [128, 256])
            fill(len(bulk))
            self.flush_silu()
        self.ps_ring = list(range(7))
        if has_extra:
            if l == self.layers[0]:
                self.ada_fix2(self.layers[0])
                self.ada_fix1(self.layers[1])
            else:
                self.ada_fix2(self.layers[1])

    def gmlp(self, ti, l):
        prm = self.prm
        hTf = lambda kc: self.hT[:, kc, :]
        for sl in range(4):
            wv = self.wload(self.w_in[l, 8 + sl], KC, 256)
            banks = self.tm(wv, KC, 256)
            for i2 in range(2):
                self.act(self.vg[:, 2 * i2:2 * i2 + 2, sl * 256:(sl + 1) * 256],
                         banks[i2][:, :].rearrange("p (c f) -> p c f", c=2), AF.Gelu)

        def ln_chunk(c):
            st = self.bnst
            for i2 in range(2):
                self.op("dve", lambda e, i2=i2, c=c: e.bn_stats(st[:, i2, 0:6], self.vg[:, c, i2 * 512:(i2 + 1) * 512]),
                        [self.vg[:, c, i2 * 512:(i2 + 1) * 512]], [st[:, i2, 0:6]])
            mv = self.sm[:, 4:6]
            self.op("dve", lambda e: e.bn_aggr(mv, st[:, :, 0:6]), [st[:, :, 0:6]], [mv])
            rs = self.sm[:, 6:7]
            self.act(rs, self.sm[:, 5:6], AF.Sqrt, bias=self.cst[:, 1:2])
            self.recip(rs, rs)
            self.ts(self.vn[:, c, :], self.vg[:, c, :], self.sm[:, 4:5], rs, ALU.subtract, ALU.mult)

        if getattr(self, "u_done", False):
            for c in range(NCH):
                ln_chunk(c)
        else:
            for sl in range(4):
                wv = self.wload(self.w_in[l, 4 + sl], KC, 256)
                banks = self.fm(wv, KC, hTf, 2)
                for j in range(2):
                    self.act(self.uF[:, sl * 2 + j, :], banks[j][:, :], AF.Gelu)
                ln_chunk(sl)
        self.u_done = False
        self.tap("vn", self.vn[:, :, :], [128, NCH, 1024], BF16)
        self.tap("uF", self.uF[:, :, :], [128, 8, T], BF16)
        for h in range(8):
            pb = self.psn()
            for c in range(NCH):
                self.mm(pb[:, c * 128:(c + 1) * 128], self.vn[:, c, h * 128:(h + 1) * 128], self.Wt[:, l, h, :], True, True)
            tmp = self.gtmp[h % 2]
            self.stt(tmp[:, :, :], pb[:, :].rearrange("p (c t) -> p c t", c=NCH), prm[:, l, 168 + h:169 + h],
                     bc(self.R[:, l, h, :], 1, [128, NCH, 128]), ALU.mult, ALU.add)
            self.tt(self.gm[:, h, :], tmp[:, :, :].rearrange("p c t -> p (c t)"), self.uF[:, h, :], ALU.mult)

    def pool(self, ti, l):
        prm = self.prm
        hTf = lambda kc: self.hT[:, kc, :]

        def stage1(g):
            wv = self.wload(self.w_in[l, g], KC, 256)
            banks = self.fm(wv, KC, hTf, 2)
            raw = self.praw
            self.cp(raw[:, :, 0:16], self.pcarry[:, l, 2 * g:2 * g + 2, :])
            for j in range(2):
                self.act(raw[:, j, 16:528], banks[j][:, :], AF.Identity)
            self.cp(self.pcarry[:, l, 2 * g:2 * g + 2, :], raw[:, :, 512:528])
            cur = raw
            bufs = [self.pA, self.pB]
            for step in range(g + 1):
                sh = 1 << step
                nx = bufs[step % 2]
                self.tt(nx[:, :, sh:528], cur[:, :, sh:528], cur[:, :, 0:528 - sh], ALU.add)
                cur = nx
            w = 2 << g
            pooled = self.pooled[g % 2]
            self.stt(pooled[:, :, :], cur[:, :, 16:528], 1.0 / w, raw[:, :, 16:528], ALU.mult, ALU.subtract)
            if ti == 0:
                self.tt(self.p16[:, :, :], cur[:, :, 16:32], bc(self.invc[:, g, :], 1, [128, 2, 16]), ALU.mult)
                self.tt(pooled[:, :, 0:16], self.p16[:, :, :], raw[:, :, 16:32], ALU.subtract)

        def stage2(g):
            pooled = self.pooled[g % 2]
            pwg = self.wload(self.pool_w[l, g], 2, 256)
            for m in range(2):
                pb = self.psn()
                for kc in range(2):
                    self.mm(pb[:, :], pwg[:, kc, m * 128:(m + 1) * 128], pooled[:, kc, :], kc == 0, kc == 1)
                self.act(self.mixed[:, 2 * g + m, :], pb[:, :], AF.Identity, scale=prm[:, l, 2 * g + m:2 * g + m + 1])

        stage1(0)
        for g in range(4):
            if g + 1 < 4:
                stage1(g + 1)
            stage2(g)

    def merge(self, ti, l):
        hTf = lambda kc: self.hT[:, kc, :]
        srcs = [(self.w_pool_out, self.mixed, 8), (self.w_gmlp_out, self.gm, 8), (self.w_ssm_out, self.yn, 16)]
        for j in range(8):
            for br in range(3):
                wv = self.wload(self.w_in[l, 36 + br * 8 + j], KC, 256)
                gb = self.fm(wv, KC, hTf, 2)
                gs = self.gsb[br % 2]
                for m in range(2):
                    self.act(gs[:, m, :], gb[m][:, :], AF.Sigmoid)
                W, src, nk = srcs[br]
                wv2 = self.wload(W[l, j], nk, 256)
                yb = self.fm(wv2, nk, lambda kc, src=src: src[:, kc, :], 2)
                tmp = self.mtmp[br % 2]
                for m in range(2):
                    if br == 0:
                        self.tt(self.macc[:, m, :], gs[:, m, :], yb[m][:, :], ALU.mult)
                    elif br == 1:
                        self.tt(tmp[:, m, :], gs[:, m, :], yb[m][:, :], ALU.mult)
                        self.tt(self.macc[:, m, :], self.macc[:, m, :], tmp[:, m, :], ALU.add)
                    else:
                        self.tt(tmp[:, m, :], gs[:, m, :], yb[m][:, :], ALU.mult)
                        self.tt(self.mg[:, 2 * j + m, :], self.macc[:, m, :], tmp[:, m, :], ALU.add)

    def wo(self, ti, l):
        for j in range(8):
            wv = self.wload(self.w_o[l, j], KC, 256)
            banks = self.fm(wv, KC, lambda kc: self.mg[:, kc, :], 2)
            for m in range(2):
                blk = 2 * j + m
                self.stt(self.xT[:, blk, :], banks[m][:, :], self.ada[:, l, 32 + blk:33 + blk], self.xT[:, blk, :],
                         ALU.mult, ALU.add)
                self.stat_acc(blk)
        self.stats_ready = True

    def ffn(self, ti, l):
        hTf = lambda kc: self.hT[:, kc, :]
        for j in range(32):
            wv = self.wload(self.w_up[l, j], KC, 256)
            banks = self.fm(wv, KC, hTf, 2)
            for m in range(2):
                r = self.rtmp[m]
                self.act(r[:, :], banks[m][:, :], AF.Relu)
                self.tt(self.hid[:, 2 * j + m, :], r[:, :], r[:, :], ALU.mult)
        for j in range(8):
            banks = [self.psn(), self.psn()]
            for kq in range(4):
                wv = self.wload(self.w_down[l, j, kq], KC, 256)
                for m in range(2):
                    for kc in range(KC):
                        self.mm(banks[m][:, :], wv[:, kc, m * 128:(m + 1) * 128], self.hid[:, kq * 16 + kc, :],
                                kq == 0 and kc == 0, kq == 3 and kc == KC - 1)
            for m in range(2):
                blk = 2 * j + m
                self.stt(self.xT[:, blk, :], banks[m][:, :], self.ada[:, l, 80 + blk:81 + blk], self.xT[:, blk, :],
                         ALU.mult, ALU.add)
                self.stat_acc(blk)
        self.stats_ready = True


def col(v, n):
    return np.ascontiguousarray(np.asarray(v, np.float32).reshape(n, 128).T)


def prep_inputs(inp):
    f = lambda a: np.ascontiguousarray(np.asarray(a, np.float32))
    sh = {}
    def tile_w(w, nk):
        w = np.asarray(w, np.float32)
        K, Fd = w.shape
        kq = K // (nk * 128)
        a = w.reshape(kq, nk, 128, Fd // 256, 256).transpose(3, 0, 2, 1, 4)
        return np.ascontiguousarray(a).reshape(Fd // 256, kq, 128, nk * 256)

    sh["w_ada_t"] = np.stack([tile_w(inp["w_ada"][l], 16)[:, 0] for l in range(DEPTH)])
    sh["b_ada_col"] = np.stack([col(inp["b_ada"][l], 96) for l in range(DEPTH)])
    perm = []
    for c0, n in ((C_P, 4), (C_U, 4), (C_V, 4), (C_Z, 8), (C_X, 8)):
        perm.extend(range(c0, c0 + n * 256))
    for g in range(8):
        perm.extend(range(C_B + g * 128, C_B + (g + 1) * 128))
        perm.extend(range(C_C + g * 128, C_C + (g + 1) * 128))
    perm.extend(range(C_G, C_G + 3 * 2048))
    perm = np.asarray(perm)
    assert perm.size == 60 * 256
    w_in = np.asarray(inp["w_in"], np.float32)
    sh["w_in_t"] = np.stack([tile_w(w_in[l][:, perm], 16)[:, 0] for l in range(DEPTH)])
    sh["w_dt_t"] = np.stack([np.ascontiguousarray(w_in[l][:, C_DT:C_DT + 32].reshape(16, 128, 32).transpose(1, 0, 2)).reshape(128, 512)
                             for l in range(DEPTH)])
    pw = np.asarray(inp["pool_w"], np.float32)
    sh["pool_w_t"] = np.ascontiguousarray(pw.reshape(DEPTH, 4, 2, 128, 256).transpose(0, 1, 3, 2, 4)).reshape(DEPTH, 4, 128, 512)
    prm = np.zeros((DEPTH, 128, 320), np.float32)
    for l in range(DEPTH):
        o = 0
        prm[l, :, 0:8] = col(inp["pool_scale"][l], 8)
        prm[l, :, 8:136] = np.asarray(inp["conv_w"][l], np.float32).T.reshape(32, 128, 4).transpose(1, 0, 2).reshape(128, 128)
        prm[l, :, 136:168] = col(inp["conv_b"][l], 32)
        prm[l, :, 168:176] = col(inp["gmlp_ln_g"][l], 8)
        prm[l, :, 176:184] = col(inp["gmlp_ln_b"][l], 8)
        prm[l, :, 184:216] = np.broadcast_to(np.asarray(inp["dt_bias"][l], np.float32)[None, :], (128, 32))
        prm[l, :, 216:248] = np.broadcast_to(np.asarray(inp["a_log"][l], np.float32)[None, :], (128, 32))
        prm[l, :, 248:280] = np.broadcast_to(np.asarray(inp["d_skip"][l], np.float32)[None, :], (128, 32))
        prm[l, :, 280:296] = col(inp["ssm_norm"][l], 16)
    sh["prm"] = prm
    sh["gmlp_wsT"] = np.ascontiguousarray(np.asarray(inp["gmlp_ws"], np.float32).transpose(0, 3, 1, 2))
    sh["gmlp_bs_bc"] = np.ascontiguousarray(np.broadcast_to(np.asarray(inp["gmlp_bs"], np.float32)[:, None, :, :], (DEPTH, 128, 8, 128)))
    for k, nk in (("w_pool_out", 8), ("w_gmlp_out", 8), ("w_ssm_out", 16), ("w_o", 16), ("w_up", 16)):
        sh[k + "_t"] = np.stack([tile_w(inp[k][l], nk)[:, 0] for l in range(DEPTH)])
    sh["w_down_t"] = np.stack([tile_w(inp["w_down"][l], 16) for l in range(DEPTH)])
    sh["final_norm_col"] = col(inp["final_norm"], KC)
    x = np.asarray(inp["x"], np.float32)
    c = np.asarray(inp["c"], np.float32)
    per = []
    for b in range(x.shape[0]):
        d = dict(sh)
        d["xT"] = np.ascontiguousarray(x[b].T)
        d["c_col"] = col(c[b], KC)
        per.append(d)
    return per


_CACHE = {}


def kernel(**inputs):
    per = prep_inputs(inputs)
    if "nc" not in _CACHE:
        _CACHE["nc"] = Builder().build()
    nc = _CACHE["nc"]
    res = run_bass_kernel_spmd(nc, per, core_ids=list(range(8)))
    out = np.stack([np.ascontiguousarray(r["outT"].T) for r in res.results], axis=0)
    return out.astype(np.float32)
```
